# Optimizing a Trainium2 kernel written in Bass

```python
import jax
import jax.numpy as jnp
from jax import lax
import numpy as np

D_MODEL = 1024
BATCH = 2
SEQ = 8192
DEPTH = 2

HEAD_DIM = 64
Q_BLOCK = 128
MIX_W = 256
N_BRANCH = 4
SB_HEADS = 4
NSA_HEADS = 4
NSA_GATES = 3
CMP_LEN = 32
CMP_STRIDE = 16
SEL_LEN = 64
SEL_TOPK = 16
WINDOW = 512
RET_HEADS = 4
RET_CHUNK = 128
GLA_HEADS = 4
GLA_DK = 32
GLA_DV = 64
GLA_LOWRANK = 16
GLA_TAU = 16.0
GLA_CHUNK = 64
PEER_HEADS = 8
N_KEYS = 128
N_EXPERTS = N_KEYS * N_KEYS
PEER_TOPK = 16
PEER_QDIM = 256
PEER_BLOCK = 128
DN_ALPHA = (2 * DEPTH) ** 0.25
DN_BETA = (8 * DEPTH) ** -0.25
LN_EPS = 1e-5
NEG = -1e30

IN_WIDTHS = (
    MIX_W, MIX_W, MIX_W,
    NSA_HEADS * HEAD_DIM,
    HEAD_DIM, HEAD_DIM, HEAD_DIM, HEAD_DIM, HEAD_DIM, HEAD_DIM,
    NSA_HEADS * NSA_GATES,
    MIX_W, MIX_W, MIX_W, MIX_W,
    GLA_HEADS * GLA_DK, GLA_HEADS * GLA_DK,
    GLA_HEADS * GLA_DV, GLA_HEADS * GLA_DV,
    GLA_LOWRANK,
    N_BRANCH * D_MODEL,
)
IN_TOTAL = sum(IN_WIDTHS)

kernel_name = 'hybrid_sb_nsa_ret_gla_peer_deepnorm'


def _layer_norm(x, g, b):
    xf = x.astype(jnp.float32)
    mu = jnp.mean(xf, axis=-1, keepdims=True)
    var = jnp.mean(jnp.square(xf - mu), axis=-1, keepdims=True)
    return ((xf - mu) * lax.rsqrt(var + LN_EPS) * g + b).astype(x.dtype)


def _split(h, widths):
    cuts = [int(c) for c in np.cumsum(widths)[:-1]]
    return jnp.split(h, cuts, axis=-1)


def _heads(t, n):
    B, S, _ = t.shape
    return t.reshape(B, S, n, -1).transpose(0, 2, 1, 3)


def _merge_heads(t):
    B, H, S, d = t.shape
    return t.transpose(0, 2, 1, 3).reshape(B, S, H * d)


def _alibi_slopes(n):
    return 2.0 ** (-8.0 * jnp.arange(1, n + 1, dtype=jnp.float32) / n)


def _masked_softmax(s, mask):
    return jax.nn.softmax(jnp.where(mask, s, NEG), axis=-1)


def _band(t, n_prev):
    B, S, D = t.shape
    nb = S // Q_BLOCK
    tp = jnp.pad(t, ((0, 0), (n_prev * Q_BLOCK, 0), (0, 0))).reshape(B, nb + n_prev, Q_BLOCK, D)
    return jnp.concatenate([tp[:, j:j + nb] for j in range(n_prev + 1)], axis=2)


def stick_breaking_attention(q, k, v):
    B, H, S, Dh = q.shape
    nb = S // Q_BLOCK
    scale = Dh ** -0.5
    s_pos = jnp.arange(S)
    qb = q.reshape(B, H, nb, Q_BLOCK, Dh).transpose(2, 0, 1, 3, 4)

    def one_block(args):
        qi, i = args
        z = jnp.einsum('bhqd,bhsd->bhqs', qi, k).astype(jnp.float32) * scale
        t_pos = i * Q_BLOCK + jnp.arange(Q_BLOCK)
        causal = s_pos[None, :] < t_pos[:, None]
        log_1mb = jnp.where(causal, jax.nn.log_sigmoid(-z), 0.0)
        between = lax.cumsum(log_1mb, axis=3, reverse=True) - log_1mb
        w = jnp.where(causal, jnp.exp(jax.nn.log_sigmoid(z) + between), 0.0)
        return jnp.einsum('bhqs,bhsd->bhqd', w.astype(v.dtype), v)

    o = lax.map(one_block, (qb, jnp.arange(nb)))
    return o.transpose(1, 2, 0, 3, 4).reshape(B, H, S, Dh)


def nsa_attention(q, k_cmp, v_cmp, k_slc, v_slc, k_win, v_win, gate_logits, cmp_pos, cmp_wk, cmp_wv):
    B, H, S, Dh = q.shape
    f32 = jnp.float32
    scale = Dh ** -0.5
    slopes = _alibi_slopes(H)[None, :, None, None]
    t_pos = jnp.arange(S)
    nb = S // Q_BLOCK
    t_blk = t_pos.reshape(nb, Q_BLOCK)

    n_cmp = (S - CMP_LEN) // CMP_STRIDE + 1
    tok = jnp.arange(n_cmp)[:, None] * CMP_STRIDE + jnp.arange(CMP_LEN)[None, :]

    def compress(kv, w):
        blk = kv[:, tok] + cmp_pos
        return blk.reshape(B, n_cmp, CMP_LEN * Dh) @ w

    kc = compress(k_cmp, cmp_wk)
    vc = compress(v_cmp, cmp_wv)
    cmp_end = jnp.arange(n_cmp) * CMP_STRIDE + CMP_LEN - 1
    d_c = (t_pos[:, None] - cmp_end[None, :]).astype(f32)
    s_c = jnp.einsum('bhtd,bcd->bhtc', q, kc).astype(f32) * scale - slopes * d_c
    any_c = (t_pos >= CMP_LEN - 1).astype(f32)[:, None]
    p_c = _masked_softmax(s_c, d_c >= 0) * any_c
    o_cmp = jnp.einsum('bhtc,bcd->bhtd', p_c.astype(vc.dtype), vc)

    n_sel = S // SEL_LEN
    c_start = jnp.arange(n_cmp) * CMP_STRIDE
    sel_start = jnp.arange(n_sel) * SEL_LEN
    overlap = ((c_start[:, None] < sel_start[None, :] + SEL_LEN)
               & (c_start[:, None] + CMP_LEN > sel_start[None, :])).astype(f32)
    imp = jnp.einsum('bhtc,cn->btn', p_c, overlap)
    imp = jnp.where(sel_start[None, :] > t_pos[:, None], NEG, imp)
    imp = jnp.where(jnp.arange(n_sel)[None, :] == (t_pos // SEL_LEN)[:, None], -NEG, imp)
    n_top = min(SEL_TOPK, n_sel)
    _, sel_idx = lax.top_k(imp, n_top)

    qb = q.reshape(B, H, nb, Q_BLOCK, Dh).transpose(2, 0, 1, 3, 4)
    ib = sel_idx.reshape(B, nb, Q_BLOCK, n_top).transpose(1, 0, 2, 3)
    b_ix = jnp.arange(B)[:, None, None]

    def sel_block(args):
        qi, ii, ti = args
        pos = (ii[..., None] * SEL_LEN + jnp.arange(SEL_LEN)).reshape(B, Q_BLOCK, n_top * SEL_LEN)
        kk = k_slc[b_ix, pos]
        vv = v_slc[b_ix, pos]
        dist = (ti[None, :, None] - pos).astype(f32)
        s = jnp.einsum('bhqd,bqld->bhql', qi, kk).astype(f32) * scale - slopes * dist[:, None]
        p = _masked_softmax(s, (dist >= 0)[:, None])
        return jnp.einsum('bhql,bqld->bhqd', p.astype(vv.dtype), vv)

    o_slc = lax.map(sel_block, (qb, ib, t_blk)).transpose(1, 2, 0, 3, 4).reshape(B, H, S, Dh)

    n_prev = WINDOW // Q_BLOCK
    Lw = (n_prev + 1) * Q_BLOCK
    kw = _band(k_win, n_prev)
    vw = _band(v_win, n_prev)
    key_pos = (jnp.arange(nb)[:, None] - n_prev) * Q_BLOCK + jnp.arange(Lw)[None, :]
    dist_w = t_blk[:, :, None] - key_pos[:, None, :]
    mask_w = (dist_w >= 0) & (dist_w < WINDOW) & (key_pos[:, None, :] >= 0)
    s_w = (jnp.einsum('bhnqd,bnld->bhnql', q.reshape(B, H, nb, Q_BLOCK, Dh), kw).astype(f32) * scale
           - slopes[..., None] * dist_w.astype(f32))
    p_w = _masked_softmax(s_w, mask_w)
    o_win = jnp.einsum('bhnql,bnld->bhnqd', p_w.astype(vw.dtype), vw).reshape(B, H, S, Dh)

    g = jax.nn.sigmoid(gate_logits.astype(f32)).reshape(B, S, H, NSA_GATES).transpose(0, 2, 1, 3).astype(q.dtype)
    return g[..., 0:1] * o_cmp + g[..., 1:2] * o_slc + g[..., 2:3] * o_win


def retention_block(q, k, v, gate, gn_g, gn_b):
    B, S, _ = q.shape
    H = RET_HEADS
    Dh = MIX_W // H
    C = RET_CHUNK
    n = S // C
    f32 = jnp.float32
    log_g = jnp.log1p(-(2.0 ** (-5.0 - jnp.arange(H, dtype=f32))))
    idx = jnp.arange(C, dtype=f32)
    diff = idx[:, None] - idx[None, :]
    decay_in = jnp.where(diff >= 0, jnp.exp(log_g[:, None, None] * jnp.maximum(diff, 0.0)), 0.0)
    q_decay = jnp.exp(log_g[:, None] * (idx + 1.0))[None, :, :, None]
    k_decay = jnp.exp(log_g[:, None] * (C - 1.0 - idx))[None, :, :, None]
    chunk_decay = jnp.exp(log_g * C)[None, :, None, None]

    def chunks(t):
        return t.reshape(B, H, n, C, Dh).transpose(2, 0, 1, 3, 4).astype(f32)

    qc = chunks(_heads(q, H))
    kc = chunks(_heads(k, H) * Dh ** -0.5)
    vc = chunks(_heads(v, H))

    def step(R, inp):
        qi, ki, vi = inp
        att = jnp.einsum('bhid,bhjd->bhij', qi, ki) * decay_in
        o = jnp.einsum('bhij,bhjd->bhid', att, vi) + jnp.einsum('bhid,bhde->bhie', qi, R) * q_decay
        R = chunk_decay * R + jnp.einsum('bhjd,bhje->bhde', ki * k_decay, vi)
        return R, o

    _, o = lax.scan(step, jnp.zeros((B, H, Dh, Dh), f32), (qc, kc, vc))
    o = o.transpose(1, 2, 0, 3, 4).reshape(B, H, S, Dh)
    mu = jnp.mean(o, axis=-1, keepdims=True)
    var = jnp.mean(jnp.square(o - mu), axis=-1, keepdims=True)
    o = _merge_heads((o - mu) * lax.rsqrt(var + LN_EPS)) * gn_g + gn_b
    return (jax.nn.silu(gate.astype(f32)) * o).astype(q.dtype)


def gla_block(q, k, v, r, lr, w_alpha, b_alpha, norm_g):
    B, S, _ = q.shape
    H = GLA_HEADS
    C = GLA_CHUNK
    n = S // C
    f32 = jnp.float32
    log_a = jax.nn.log_sigmoid((lr @ w_alpha + b_alpha).astype(f32)) / GLA_TAU

    def chunks(t, d):
        return t.reshape(B, n, C, H, d).transpose(1, 0, 3, 2, 4).astype(f32)

    qc = chunks(q * GLA_DK ** -0.5, GLA_DK)
    kc = chunks(k, GLA_DK)
    vc = chunks(v, GLA_DV)
    ac = chunks(log_a, GLA_DK)
    causal = jnp.tril(jnp.ones((C, C), dtype=bool))[:, :, None]

    def step(St, inp):
        qi, ki, vi, ai = inp
        b = jnp.cumsum(ai, axis=2)
        inter = jnp.einsum('bhtk,bhkv->bhtv', qi * jnp.exp(b), St)
        diff = b[:, :, :, None, :] - b[:, :, None, :, :]
        decay = jnp.where(causal, jnp.exp(jnp.where(causal, diff, 0.0)), 0.0)
        att = jnp.einsum('bhtk,bhsk,bhtsk->bhts', qi, ki, decay)
        intra = jnp.einsum('bhts,bhsv->bhtv', att, vi)
        b_last = b[:, :, -1, :]
        St = jnp.exp(b_last)[..., None] * St + jnp.einsum('bhsk,bhsv->bhkv', ki * jnp.exp(b_last[:, :, None, :] - b), vi)
        return St, inter + intra

    _, o = lax.scan(step, jnp.zeros((B, H, GLA_DK, GLA_DV), f32), (qc, kc, vc, ac))
    o = o.transpose(1, 0, 3, 2, 4).reshape(B, S, H, GLA_DV)
    o = (o * lax.rsqrt(jnp.mean(jnp.square(o), axis=-1, keepdims=True) + LN_EPS)).reshape(B, S, H * GLA_DV) * norm_g
    return (jax.nn.silu(r.astype(f32)) * o).astype(q.dtype)


def peer_ffn(x, wq, sub_keys, u_tab, v_tab):
    B, S, D = x.shape
    nb = S // PEER_BLOCK
    half = PEER_QDIM // 2
    K = PEER_TOPK
    xb = x.reshape(B, nb, PEER_BLOCK, D).transpose(1, 0, 2, 3)

    def block(xi):
        qv = (xi @ wq).reshape(B, PEER_BLOCK, PEER_HEADS, 2, half)
        s = jnp.einsum('bthpc,hpnc->bthpn', qv, sub_keys).astype(jnp.float32)
        top_s, top_i = lax.top_k(s, K)
        cand_s = (top_s[..., 0, :, None] + top_s[..., 1, None, :]).reshape(B, PEER_BLOCK, PEER_HEADS, K * K)
        cand_i = (top_i[..., 0, :, None] * N_KEYS + top_i[..., 1, None, :]).reshape(B, PEER_BLOCK, PEER_HEADS, K * K)
        best_s, best_j = lax.top_k(cand_s, K)
        expert = jnp.take_along_axis(cand_i, best_j, axis=-1)
        gate = jax.nn.softmax(best_s, axis=-1)
        act = jax.nn.gelu(jnp.einsum('btd,bthkd->bthk', xi, u_tab[expert]).astype(jnp.float32), approximate=False)
        return jnp.einsum('bthk,bthkd->btd', (gate * act).astype(xi.dtype), v_tab[expert])

    return lax.map(block, xb).transpose(1, 0, 2, 3).reshape(B, S, D)


def hybrid_layer(x, w_in, cmp_pos, cmp_wk, cmp_wv, ret_gn_g, ret_gn_b, gla_w_alpha, gla_b_alpha,
                 gla_norm_g, w_branch, w_out, ln1_g, ln1_b, peer_wq, peer_keys, peer_u, peer_v, ln2_g, ln2_b):
    B, S, D = x.shape
    f32 = jnp.float32
    (sb_q, sb_k, sb_v, n_q, n_kc, n_vc, n_ks, n_vs, n_kw, n_vw, n_g,
     r_q, r_k, r_v, r_g, g_q, g_k, g_v, g_r, g_lr, m_g) = _split(x @ w_in, IN_WIDTHS)

    o_a = _merge_heads(stick_breaking_attention(_heads(sb_q, SB_HEADS), _heads(sb_k, SB_HEADS), _heads(sb_v, SB_HEADS)))
    o_b = _merge_heads(nsa_attention(_heads(n_q, NSA_HEADS), n_kc, n_vc, n_ks, n_vs, n_kw, n_vw, n_g,
                                     cmp_pos, cmp_wk, cmp_wv))
    o_c = retention_block(r_q, r_k, r_v, r_g, ret_gn_g, ret_gn_b)
    o_d = gla_block(g_q, g_k, g_v, g_r, g_lr, gla_w_alpha, gla_b_alpha, gla_norm_g)

    branches = jnp.stack([o_a, o_b, o_c, o_d], axis=2)
    proj = jnp.einsum('bsnc,ncd->bsnd', branches, w_branch)
    gates = jax.nn.sigmoid(m_g.astype(f32)).reshape(B, S, N_BRANCH, D).astype(x.dtype)
    mix = jnp.sum(gates * proj, axis=2) @ w_out
    x = _layer_norm(DN_ALPHA * x + mix, ln1_g, ln1_b)
    x = _layer_norm(DN_ALPHA * x + peer_ffn(x, peer_wq, peer_keys, peer_u, peer_v), ln2_g, ln2_b)
    return x


def setup_inputs(seed: int = 0) -> dict:
    key = jax.random.key(seed)
    ks = jax.random.split(key, 24)
    f32 = jnp.float32
    L = DEPTH

    def nrm(k, shape, scale):
        return jax.random.normal(k, shape, f32) * scale

    return {
        'x': nrm(ks[0], (BATCH, SEQ, D_MODEL), 1.0),
        'ln_in_g': 1.0 + nrm(ks[1], (D_MODEL,), 0.02),
        'ln_in_b': nrm(ks[2], (D_MODEL,), 0.02),
        'w_in': nrm(ks[3], (L, D_MODEL, IN_TOTAL), D_MODEL ** -0.5),
        'nsa_cmp_pos': nrm(ks[4], (L, CMP_LEN, HEAD_DIM), 0.1),
        'nsa_cmp_wk': nrm(ks[5], (L, CMP_LEN * HEAD_DIM, HEAD_DIM), (CMP_LEN * HEAD_DIM) ** -0.5),
        'nsa_cmp_wv': nrm(ks[6], (L, CMP_LEN * HEAD_DIM, HEAD_DIM), (CMP_LEN * HEAD_DIM) ** -0.5),
        'ret_gn_g': 1.0 + nrm(ks[7], (L, MIX_W), 0.02),
        'ret_gn_b': nrm(ks[8], (L, MIX_W), 0.02),
        'gla_w_alpha': nrm(ks[9], (L, GLA_LOWRANK, GLA_HEADS * GLA_DK), GLA_LOWRANK ** -0.5),
        'gla_b_alpha': nrm(ks[10], (L, GLA_HEADS * GLA_DK), 0.1),
        'gla_norm_g': 1.0 + nrm(ks[11], (L, GLA_HEADS * GLA_DV), 0.02),
        'w_branch': nrm(ks[12], (L, N_BRANCH, MIX_W, D_MODEL), DN_BETA * MIX_W ** -0.5),
        'w_out': nrm(ks[13], (L, D_MODEL, D_MODEL), DN_BETA * D_MODEL ** -0.5),
        'ln1_g': 1.0 + nrm(ks[14], (L, D_MODEL), 0.02),
        'ln1_b': nrm(ks[15], (L, D_MODEL), 0.02),
        'peer_wq': nrm(ks[16], (L, D_MODEL, PEER_HEADS * PEER_QDIM), D_MODEL ** -0.5),
        'peer_keys': nrm(ks[17], (L, PEER_HEADS, 2, N_KEYS, PEER_QDIM // 2), (PEER_QDIM // 2) ** -0.5),
        'peer_u': nrm(ks[18], (L, N_EXPERTS, D_MODEL), D_MODEL ** -0.5),
        'peer_v': nrm(ks[19], (L, N_EXPERTS, D_MODEL), DN_BETA * PEER_HEADS ** -0.5),
        'ln2_g': 1.0 + nrm(ks[20], (L, D_MODEL), 0.02),
        'ln2_b': nrm(ks[21], (L, D_MODEL), 0.02),
    }


def reference(x, ln_in_g, ln_in_b, w_in, nsa_cmp_pos, nsa_cmp_wk, nsa_cmp_wv, ret_gn_g, ret_gn_b,
              gla_w_alpha, gla_b_alpha, gla_norm_g, w_branch, w_out, ln1_g, ln1_b,
              peer_wq, peer_keys, peer_u, peer_v, ln2_g, ln2_b):
    h = _layer_norm(x, ln_in_g, ln_in_b)
    for l in range(DEPTH):
        h = hybrid_layer(h, w_in[l], nsa_cmp_pos[l], nsa_cmp_wk[l], nsa_cmp_wv[l], ret_gn_g[l], ret_gn_b[l],
                         gla_w_alpha[l], gla_b_alpha[l], gla_norm_g[l], w_branch[l], w_out[l],
                         ln1_g[l], ln1_b[l], peer_wq[l], peer_keys[l], peer_u[l], peer_v[l],
                         ln2_g[l], ln2_b[l])
    return h
```

```python
import contextlib
import numpy as np
import concourse.bass as bass
import concourse.mybir as mybir
from concourse.bass_utils import run_bass_kernel_spmd

F32 = mybir.dt.float32
BF16 = mybir.dt.bfloat16
AF = mybir.ActivationFunctionType
ALU = mybir.AluOpType
AX = mybir.AxisListType

NCORES = 8
D = 1024
SEQ = 8192
BATCH = 2
IN_TOTAL = 7324
LN_EPS = 1e-5
DN_ALPHA = 4.0 ** 0.25


class Prog:
    ENG = ('pe', 'dve', 'act', 'pool', 'sp')

    def __init__(self, nc, stack):
        self.nc = nc
        self.stack = stack
        self.e = dict(pe=nc.tensor, dve=nc.vector, act=nc.scalar, pool=nc.gpsimd, sp=nc.sync)
        self.semh = {}
        self.cnt = {}
        for k in self.ENG:
            self.semh[k] = stack.enter_context(nc.semaphore('s_' + k))
            self.cnt[k] = 0
        self.seen = {k: {} for k in self.ENG}
        self.W = {}
        self.R = {}
        self.nuniq = 0

    def sb(self, name, shape, dt=F32):
        return self.stack.enter_context(self.nc.sbuf_tensor(name, list(shape), dt))

    def ps(self, name, shape, dt=F32):
        return self.stack.enter_context(self.nc.psum_tensor(name, list(shape), dt))

    def _group(self, g):
        if g not in self.semh:
            self.semh[g] = self.stack.enter_context(self.nc.semaphore('g_' + g))
            self.cnt[g] = 0
        return g

    def _wait(self, eng, reads, writes):
        need = {}
        for t in reads:
            for s, c in self.W.get(t, {}).items():
                if s == eng and eng == 'pe':
                    continue
                need[s] = max(need.get(s, 0), c)
        for t in writes:
            for s, c in self.W.get(t, {}).items():
                if s == eng:
                    continue
                need[s] = max(need.get(s, 0), c)
            for s, c in self.R.get(t, {}).items():
                if s == eng:
                    continue
                need[s] = max(need.get(s, 0), c)
        for s, c in need.items():
            if self.seen[eng].get(s, 0) < c:
                self.e[eng].wait_ge(self.semh[s], c)
                self.seen[eng][s] = c

    def _mark(self, s, c, reads, writes):
        for t in writes:
            self.W[t] = {s: c}
            self.R[t] = {}
        for t in reads:
            d = self.R.setdefault(t, {})
            d[s] = max(d.get(s, 0), c)

    def op(self, eng, fn, reads=(), writes=()):
        self._wait(eng, reads, writes)
        ins = fn(self.e[eng])
        self.cnt[eng] += 1
        ins.then_inc(self.semh[eng], 1)
        self._mark(eng, self.cnt[eng], reads, writes)
        return ins

    def dma(self, eng, out, in_, reads=(), writes=(), group=None, **kw):
        g = self._group(group)
        self._wait(eng, reads, writes)
        ins = self.e[eng].dma_start(out=out, in_=in_, **kw)
        self.cnt[g] += 16
        ins.then_inc(self.semh[g], 16)
        self._mark(g, self.cnt[g], reads, writes)
        return ins

    def finish(self, eng='sp'):
        for s, c in self.cnt.items():
            if c > 0 and self.seen[eng].get(s, 0) < c:
                self.e[eng].wait_ge(self.semh[s], c)
                self.seen[eng][s] = c


def bcast_rows(ap_1d, nparts):
    return ap_1d.partition_broadcast(nparts)


def build_proj(T, do_ln, NCOL=IN_TOTAL):
    nc = bass.Bass("TRN2", target_bir_lowering=False)
    NT = T // 128
    KC = D // 128
    x = nc.dram_tensor("x", [T, D], F32, kind="ExternalInput").ap()
    w = nc.dram_tensor("w", [D, NCOL], F32, kind="ExternalInput").ap()
    ident_d = nc.dram_tensor("ident", [128, 128], F32, kind="ExternalInput").ap()
    if do_ln:
        g = nc.dram_tensor("g", [D], F32, kind="ExternalInput").ap()
        b = nc.dram_tensor("b", [D], F32, kind="ExternalInput").ap()
        hout = nc.dram_tensor("h", [T, D], F32, kind="ExternalOutput").ap()
    pout = nc.dram_tensor("p", [T, NCOL], F32, kind="ExternalOutput").ap()

    with contextlib.ExitStack() as st:
        P = Prog(nc, st)
        ident = P.sb("ident_sb", [128, 128], F32)
        identb = P.sb("identb", [128, 128], BF16)
        hT = P.sb("hT", [128, KC, T], BF16)
        xt = [P.sb(f"xt{i}", [128, D], F32) for i in range(2)]
        hb = [P.sb(f"hb{i}", [128, D], BF16) for i in range(2)]
        if do_ln:
            gb = P.sb("gb", [128, D], F32)
            bb = P.sb("bb", [128, D], F32)
            stats = [P.sb(f"stats{i}", [128, 2, 6], F32) for i in range(2)]
            mv = [P.sb(f"mv{i}", [128, 2], F32) for i in range(2)]
            rstd = [P.sb(f"rstd{i}", [128, 1], F32) for i in range(2)]
        tp = [P.ps(f"tp{i}", [128, 4, 128], BF16) for i in range(2)]
        epsb = P.sb("epsb", [128, 1], F32)
        P.op('pool', lambda e: e.memset(epsb[:], LN_EPS), writes=['epsb'])

        P.dma('sp', ident[:], ident_d, writes=['ident'], group='const')
        P.op('dve', lambda e: e.tensor_copy(out=identb[:], in_=ident[:]), reads=['ident'], writes=['identb'])
        if do_ln:
            P.dma('sp', gb[:], bcast_rows(g, 128), writes=['gb'], group='const')
            P.dma('sp', bb[:], bcast_rows(b, 128), writes=['bb'], group='const')

        for i in range(NT):
            s = i % 2
            P.dma('sp', xt[s][:], x[i * 128:(i + 1) * 128, :], writes=[f'xt{s}'], group=f'xt{s}')
            if do_ln:
                xr = xt[s][:].rearrange("p (c f) -> p c f", c=2)
                for c in range(2):
                    P.op('dve', lambda e, c=c: e.bn_stats(out=stats[s][:, c, :], in_=xr[:, c, :]),
                         reads=[f'xt{s}'], writes=[f'stats{s}'])
                P.op('dve', lambda e: e.bn_aggr(out=mv[s][:], in_=stats[s][:]),
                     reads=[f'stats{s}'], writes=[f'mv{s}'])
                P.op('act', lambda e: e.activation(out=rstd[s][:], in_=mv[s][:, 1:2], func=AF.Sqrt, bias=epsb[:, 0:1], scale=1.0),
                     reads=[f'mv{s}', 'epsb'], writes=[f'rstd{s}'])
                P.op('dve', lambda e: e.reciprocal(out=rstd[s][:], in_=rstd[s][:]),
                     reads=[f'rstd{s}'], writes=[f'rstd{s}'])
                P.op('dve', lambda e: e.tensor_scalar(out=xt[s][:], in0=xt[s][:], scalar1=mv[s][:, 0:1],
                                                      scalar2=rstd[s][:, 0:1], op0=ALU.subtract, op1=ALU.mult),
                     reads=[f'xt{s}', f'mv{s}', f'rstd{s}'], writes=[f'xt{s}'])
                P.op('pool', lambda e: e.tensor_tensor(out=xt[s][:], in0=xt[s][:], in1=gb[:], op=ALU.mult),
                     reads=[f'xt{s}', 'gb'], writes=[f'xt{s}'])
                P.op('pool', lambda e: e.tensor_tensor(out=xt[s][:], in0=xt[s][:], in1=bb[:], op=ALU.add),
                     reads=[f'xt{s}', 'bb'], writes=[f'xt{s}'])
                P.dma('pool', hout[i * 128:(i + 1) * 128, :], xt[s][:], reads=[f'xt{s}'], group=f'ho{s}')
            P.op('act', lambda e: e.copy(out=hb[s][:], in_=xt[s][:]), reads=[f'xt{s}'], writes=[f'hb{s}'])
            for half in range(2):
                for c in range(4):
                    kc = half * 4 + c
                    P.op('pe', lambda e, kc=kc, c=c: e.transpose(out=tp[half][:, c, :], in_=hb[s][:, kc * 128:(kc + 1) * 128],
                                                                identity=identb[:]),
                         reads=[f'hb{s}', 'identb'], writes=[f'tp{half}'])
                eng = 'dve' if half == 0 else 'act'
                if eng == 'dve':
                    P.op('dve', lambda e, half=half: e.tensor_copy(out=hT[:, half * 4:(half + 1) * 4, i * 128:(i + 1) * 128],
                                                                   in_=tp[half][:]),
                         reads=[f'tp{half}'], writes=['hT'])
                else:
                    P.op('act', lambda e, half=half: e.copy(out=hT[:, half * 4:(half + 1) * 4, i * 128:(i + 1) * 128],
                                                            in_=tp[half][:]),
                         reads=[f'tp{half}'], writes=['hT'])

        CB = 512
        ncb = (NCOL + CB - 1) // CB
        wf = [P.sb(f"wf{i}", [128, KC, CB], F32) for i in range(2)]
        wb = [P.sb(f"wb{i}", [128, KC, CB], BF16) for i in range(2)]
        acc = [P.ps(f"acc{i}", [128, CB], F32) for i in range(4)]
        ob = [P.sb(f"ob{i}", [128, CB], F32) for i in range(4)]
        wv = w.rearrange("(c p) n -> p c n", p=128)
        k = 0
        for cb in range(ncb):
            s = cb % 2
            c0 = cb * CB
            cw = min(CB, NCOL - c0)
            P.dma('sp', wf[s][:, :, :cw], wv[:, :, c0:c0 + cw], writes=[f'wf{s}'], group=f'wf{s}')
            P.op('pool', lambda e: e.tensor_copy(out=wb[s][:, :, :cw], in_=wf[s][:, :, :cw]),
                 reads=[f'wf{s}'], writes=[f'wb{s}'])
            for i in range(NT):
                a = k % 4
                k += 1
                for kc in range(KC):
                    P.op('pe', lambda e, kc=kc: e.matmul(acc[a][:, :cw], lhsT=hT[:, kc, i * 128:(i + 1) * 128],
                                                         rhs=wb[s][:, kc, :cw], start=(kc == 0), stop=(kc == KC - 1)),
                         reads=['hT', f'wb{s}'], writes=[f'acc{a}'])
                if a % 2 == 0:
                    P.op('dve', lambda e: e.tensor_copy(out=ob[a][:, :cw], in_=acc[a][:, :cw]),
                         reads=[f'acc{a}'], writes=[f'ob{a}'])
                else:
                    P.op('act', lambda e: e.copy(out=ob[a][:, :cw], in_=acc[a][:, :cw]),
                         reads=[f'acc{a}'], writes=[f'ob{a}'])
                P.dma('act' if a % 2 == 0 else 'pool', pout[i * 128:(i + 1) * 128, c0:c0 + cw], ob[a][:, :cw],
                      reads=[f'ob{a}'], group=f'ob{a}')
        P.finish('sp')
    return nc


_IDENT = np.eye(128, dtype=np.float32)


def run_proj(x2d, w, g=None, b=None):
    N = x2d.shape[0]
    T = N // NCORES
    do_ln = g is not None
    ncol = w.shape[1]
    nc = build_proj(T, do_ln, ncol)
    in_maps = []
    for c in range(NCORES):
        m = {"x": np.ascontiguousarray(x2d[c * T:(c + 1) * T]), "w": w, "ident": _IDENT}
        if do_ln:
            m["g"] = g
            m["b"] = b
        in_maps.append(m)
    res = run_bass_kernel_spmd(nc, in_maps, core_ids=list(range(NCORES)))
    Pm = np.concatenate([r["p"] for r in res.results], axis=0)
    h = np.concatenate([r["h"] for r in res.results], axis=0) if do_ln else x2d
    return h, Pm


class Rot:
    def __init__(self, P, name, shape, dt, n, psum=False):
        self.bufs = [(P.ps if psum else P.sb)(f"{name}{i}", shape, dt) for i in range(n)]
        self.name = name
        self.i = -1

    def next(self):
        self.i = (self.i + 1) % len(self.bufs)
        return self.bufs[self.i], f"{self.name}{self.i}"


def barrier(P):
    for eng in P.ENG:
        for s, c in P.cnt.items():
            if s != eng and c > 0 and P.seen[eng].get(s, 0) < c:
                P.e[eng].wait_ge(P.semh[s], c)
                P.seen[eng][s] = c


def load_w_bf16(P, dst, dkeys, src, KC, N, blk=512):
    sv = src.rearrange("(c p) n -> p c n", p=128)
    for c0 in range(0, N, blk):
        cw = min(blk, N - c0)
        P.dma('pool', dst[:, :, c0:c0 + cw], sv[:, :, c0:c0 + cw], writes=dkeys, group='wld_' + dkeys[0])


def emit_transpose8(P, src_bf, skey, dst, dkey, tp, tpkey, identb, copy_eng='dve'):
    for kc in range(8):
        P.op('pe', lambda e, kc=kc: e.transpose(out=tp[:, kc, :], in_=src_bf[:, kc * 128:(kc + 1) * 128], identity=identb[:]),
             reads=[skey, 'identb'], writes=[tpkey])
    if copy_eng == 'act':
        P.op('act', lambda e: e.copy(out=dst, in_=tp[:]), reads=[tpkey], writes=[dkey])
    else:
        P.op(copy_eng, lambda e: e.tensor_copy(out=dst, in_=tp[:]), reads=[tpkey], writes=[dkey])


def emit_ln(P, y, ykey, gb, bb, epsb, sc, sckey):
    yr = y[:].rearrange("p (c f) -> p c f", c=2)
    st6 = sc[:, 0:12].rearrange("p (c f) -> p c f", c=2)
    for c in range(2):
        P.op('dve', lambda e, c=c: e.bn_stats(out=st6[:, c, :], in_=yr[:, c, :]), reads=[ykey], writes=[sckey + 's'])
    P.op('dve', lambda e: e.bn_aggr(out=sc[:, 12:14], in_=sc[:, 0:12]), reads=[sckey + 's'], writes=[sckey + 'm'])
    P.op('act', lambda e: e.activation(out=sc[:, 14:15], in_=sc[:, 13:14], func=AF.Sqrt, bias=epsb[:, 0:1], scale=1.0),
         reads=[sckey + 'm', 'epsb'], writes=[sckey + 'r'])
    P.op('dve', lambda e: e.reciprocal(out=sc[:, 15:16], in_=sc[:, 14:15]), reads=[sckey + 'r'], writes=[sckey + 'q'])
    P.op('dve', lambda e: e.tensor_scalar(out=y[:], in0=y[:], scalar1=sc[:, 12:13], scalar2=sc[:, 15:16],
                                          op0=ALU.subtract, op1=ALU.mult),
         reads=[ykey, sckey + 'm', sckey + 'q'], writes=[ykey])
    P.op('pool', lambda e: e.tensor_tensor(out=y[:], in0=y[:], in1=gb[:], op=ALU.mult), reads=[ykey, 'lng'], writes=[ykey])
    P.op('pool', lambda e: e.tensor_tensor(out=y[:], in0=y[:], in1=bb[:], op=ALU.add), reads=[ykey, 'lnb'], writes=[ykey])


NEXP = 16384
NEG_BIG = -1.0e30


def build_tail(T):
    nc = bass.Bass("TRN2", target_bir_lowering=False)
    NT = T // 128
    TG = min(4, NT)
    NG = NT // TG
    GT = TG * 128
    EB = 512
    NI = EB // 128
    NB = NEXP // EB
    dr = lambda name, shape, kind="ExternalInput": nc.dram_tensor(name, shape, F32, kind=kind).ap()
    x = dr("x", [T, D])
    brT = dr("brT", [D, T])
    wg = dr("wg", [D, 4 * D])
    wbr = dr("wbr", [D, D])
    wout = dr("wout", [D, D])
    ln1g = dr("ln1g", [D]); ln1b = dr("ln1b", [D]); ln2g = dr("ln2g", [D]); ln2b = dr("ln2b", [D])
    wq = dr("wq", [D, 2048])
    keysT = dr("keysT", [128, 16 * 128])
    UT = dr("UT", [D, NEXP])
    V = dr("V", [NEXP, D])
    ident_d = dr("ident", [128, 128])
    x1d = dr("x1", [T, D], kind="ExternalOutput")
    out = dr("out", [T, D], kind="ExternalOutput")

    with contextlib.ExitStack() as st0:
        P = Prog(nc, st0)
        ident = P.sb("ident_sb", [128, 128], F32)
        identb = P.sb("identb", [128, 128], BF16)
        epsb = P.sb("epsb", [128, 1], F32)
        x1T = P.sb("x1T", [128, 8, T], BF16)
        P.op('pool', lambda e: e.memset(epsb[:], LN_EPS), writes=['epsb'])
        P.dma('sp', ident[:], ident_d, writes=['ident'], group='const')
        P.op('dve', lambda e: e.tensor_copy(out=identb[:], in_=ident[:]), reads=['ident'], writes=['identb'])

        with contextlib.ExitStack() as st1:
            P.stack = st1
            gb = P.sb("gb1", [128, D]); bb = P.sb("bb1", [128, D])
            P.dma('sp', gb[:], bcast_rows(ln1g, 128), writes=['lng'], group='const')
            P.dma('sp', bb[:], bcast_rows(ln1b, 128), writes=['lnb'], group='const')
            wg_bf = P.sb("wg_bf", [128, 8, 4 * D], BF16)
            wbr_bf = P.sb("wbr_bf", [128, 8, D], BF16)
            wout_bf = P.sb("wout_bf", [128, 8, D], BF16)
            load_w_bf16(P, wbr_bf[:], ['wbr'], wbr, 8, D)
            load_w_bf16(P, wg_bf[:], ['wg'], wg, 8, 4 * D)
            load_w_bf16(P, wout_bf[:], ['wout'], wout, 8, D)
            xt = Rot(P, "xt", [128, D], F32, 2)
            xb = Rot(P, "xb", [128, D], BF16, 2)
            xT = Rot(P, "xT", [128, 8, 128], BF16, 2)
            brb = Rot(P, "brb", [128, 8, 128], BF16, 2)
            sg = Rot(P, "sg", [128, 512], F32, 2)
            tmp = Rot(P, "tmp", [128, 512], F32, 2)
            mixin = Rot(P, "mixin", [128, D], F32, 2)
            mb = Rot(P, "mb", [128, D], BF16, 2)
            mT = Rot(P, "mT", [128, 8, 128], BF16, 2)
            yt = Rot(P, "yt", [128, D], F32, 2)
            yb = Rot(P, "yb", [128, D], BF16, 2)
            sc = Rot(P, "sc", [128, 16], F32, 2)
            tp = Rot(P, "tp", [128, 8, 128], BF16, 2, psum=True)
            gps = Rot(P, "gps", [128, 512], F32, 2, psum=True)
            pps = Rot(P, "pps", [128, 512], F32, 2, psum=True)
            mps = Rot(P, "mps", [128, 512], F32, 2, psum=True)
            brv = brT.rearrange("(k p) t -> p k t", p=128)
            for i in range(NT):
                tsl = slice(i * 128, (i + 1) * 128)
                xtt, xk = xt.next()
                P.dma('sp', xtt[:], x[tsl, :], writes=[xk], group=xk)
                bb_, bbk = brb.next()
                P.dma('pool', bb_[:], brv[:, :, tsl], writes=[bbk], group=bbk)
                xbb, xbk = xb.next()
                P.op('act', lambda e: e.copy(out=xbb[:], in_=xtt[:]), reads=[xk], writes=[xbk])
                xTt, xTk = xT.next()
                tpt, tpk = tp.next()
                emit_transpose8(P, xbb, xbk, xTt[:], xTk, tpt, tpk, identb)
                mx, mk = mixin.next()
                for n in range(4):
                    for half in range(2):
                        g_, gk = gps.next()
                        c0 = n * D + half * 512
                        for kc in range(8):
                            P.op('pe', lambda e, kc=kc: e.matmul(g_[:], lhsT=xTt[:, kc, :], rhs=wg_bf[:, kc, c0:c0 + 512],
                                                                 start=(kc == 0), stop=(kc == 7)),
                                 reads=[xTk, 'wg'], writes=[gk])
                        s_, sk = sg.next()
                        P.op('act', lambda e: e.activation(out=s_[:], in_=g_[:], func=AF.Sigmoid), reads=[gk], writes=[sk])
                        p_, pk = pps.next()
                        for cc in range(2):
                            P.op('pe', lambda e, cc=cc: e.matmul(p_[:], lhsT=bb_[:, n * 2 + cc, :],
                                                                 rhs=wbr_bf[:, n * 2 + cc, half * 512:(half + 1) * 512],
                                                                 start=(cc == 0), stop=(cc == 1)),
                                 reads=[bbk, 'wbr'], writes=[pk])
                        hs = slice(half * 512, (half + 1) * 512)
                        mkh = mk + str(half)
                        if n == 0:
                            P.op('dve', lambda e: e.tensor_tensor(out=mx[:, hs], in0=s_[:], in1=p_[:], op=ALU.mult),
                                 reads=[sk, pk], writes=[mkh])
                        else:
                            t_, tk = tmp.next()
                            P.op('dve', lambda e: e.tensor_tensor(out=t_[:], in0=s_[:], in1=p_[:], op=ALU.mult),
                                 reads=[sk, pk], writes=[tk])
                            P.op('pool', lambda e: e.tensor_tensor(out=mx[:, hs], in0=mx[:, hs], in1=t_[:], op=ALU.add),
                                 reads=[tk, mkh], writes=[mkh])
                mbb, mbk = mb.next()
                P.op('act', lambda e: e.copy(out=mbb[:], in_=mx[:]), reads=[mk + '0', mk + '1'], writes=[mbk])
                mTt, mTk = mT.next()
                tpt, tpk = tp.next()
                emit_transpose8(P, mbb, mbk, mTt[:], mTk, tpt, tpk, identb)
                y_, yk = yt.next()
                for half in range(2):
                    m_, mpk = mps.next()
                    for kc in range(8):
                        P.op('pe', lambda e, kc=kc: e.matmul(m_[:], lhsT=mTt[:, kc, :], rhs=wout_bf[:, kc, half * 512:(half + 1) * 512],
                                                             start=(kc == 0), stop=(kc == 7)),
                             reads=[mTk, 'wout'], writes=[mpk])
                    hs = slice(half * 512, (half + 1) * 512)
                    P.op('dve', lambda e: e.scalar_tensor_tensor(out=y_[:, hs], in0=xtt[:, hs], scalar=DN_ALPHA, in1=m_[:],
                                                                 op0=ALU.mult, op1=ALU.add),
                         reads=[xk, mpk], writes=[yk])
                sc_, sck = sc.next()
                emit_ln(P, y_, yk, gb, bb, epsb, sc_, sck)
                P.dma('pool', x1d[tsl, :], y_[:], reads=[yk], group=yk + 'o')
                ybb, ybk = yb.next()
                P.op('act', lambda e: e.copy(out=ybb[:], in_=y_[:]), reads=[yk], writes=[ybk])
                tpt, tpk = tp.next()
                emit_transpose8(P, ybb, ybk, x1T[:, :, tsl], 'x1T', tpt, tpk, identb)
            barrier(P)
        with contextlib.ExitStack() as st2:
            P.stack = st2
            gb = P.sb("gb2", [128, D]); bb = P.sb("bb2", [128, D])
            P.dma('sp', gb[:], bcast_rows(ln2g, 128), writes=['lng'], group='const')
            P.dma('sp', bb[:], bcast_rows(ln2b, 128), writes=['lnb'], group='const')
            UV = P.sb("UV", [128, 16384], BF16)
            wq_bf = UV[:, :].rearrange("p (c n) -> p c n", c=8)
            UVK = ['Ub0', 'Ub1', 'Vb0', 'Vb1']
            kT_bf = P.sb("kT_bf", [128, 16, 128], BF16)
            P.dma('pool', kT_bf[:].rearrange("p a b -> p (a b)"), keysT, writes=['kT'], group='const')
            qvT = P.sb("qvT", [128, 16, GT], BF16)
            s_sb = [P.sb(f"s_sb{t}", [128, 16, 128], F32) for t in range(TG)]
            top = [P.sb(f"top{t}", [128, 16, 16], F32) for t in range(TG)]
            best = [P.sb(f"best{t}", [128, 8, 16], F32) for t in range(TG)]
            nb = [P.sb(f"nb{t}", [128, 8], F32) for t in range(TG)]
            cand = P.sb("cand", [128, 8, 256], F32)
            scr = P.sb("scr", [128, 256], F32)
            ez = P.sb("ez", [128, 8, 16], F32)
            zz = P.sb("zz", [128, 8], F32)
            yacc = [P.sb(f"yacc{t}", [128, D], F32) for t in range(TG)]
            gel = [P.sb(f"gel{t}", [128, EB], BF16) for t in range(TG)]
            Sb = Rot(P, "S", [128, NI, 128], F32, 2)
            Eb = Rot(P, "E", [128, EB], F32, 2)
            Wb = Rot(P, "W", [128, EB], F32, 2)
            Gb = Rot(P, "G", [128, EB], F32, 2)
            GAb = Rot(P, "GA", [128, EB], BF16, 2)
            GATb = Rot(P, "GAT", [128, NI, 128], BF16, 2)
            xr = Rot(P, "xr", [128, D], F32, 2)
            sc = Rot(P, "sc2", [128, 16], F32, 2)
            qps = Rot(P, "qps", [128, 512], F32, 1, psum=True)
            sps = Rot(P, "sps", [128, 4, 128], F32, 1, psum=True)
            hps = Rot(P, "hps", [128, 512], F32, 2, psum=True)
            tpg = Rot(P, "tpg", [128, NI, 128], BF16, 2, psum=True)
            yps = Rot(P, "yps", [128, 512], F32, 2, psum=True)
            UTv = UT.rearrange("(c p) n -> p c n", p=128)
            Vv = V.rearrange("(b j p) d -> b p j d", p=128, j=NI)
            ubuf = [UV[:, i * 4096:(i + 1) * 4096].rearrange("p (c n) -> p c n", c=8) for i in range(2)]
            vbuf = [UV[:, 8192 + i * 4096:8192 + (i + 1) * 4096].rearrange("p (j d) -> p j d", j=NI) for i in range(2)]

            def load_uv(eb):
                i = eb % 2
                P.dma('pool', ubuf[i], UTv[:, :, eb * EB:(eb + 1) * EB], writes=[f'Ub{i}'], group=f'Ub{i}')
                P.dma('pool', vbuf[i], Vv[eb], writes=[f'Vb{i}'], group=f'Vb{i}')

            for g in range(NG):
                g0 = g * GT
                load_w_bf16(P, wq_bf, UVK, wq, 8, 2048)
                for hp in range(16):
                    q_, qk = qps.next()
                    for kc in range(8):
                        P.op('pe', lambda e, kc=kc: e.matmul(q_[:, :GT], lhsT=wq_bf[:, kc, hp * 128:(hp + 1) * 128],
                                                             rhs=x1T[:, kc, g0:g0 + GT], start=(kc == 0), stop=(kc == 7)),
                             reads=UVK + ['x1T'], writes=[qk])
                    if hp % 2 == 0:
                        P.op('dve', lambda e: e.tensor_copy(out=qvT[:, hp, :], in_=q_[:, :GT]), reads=[qk], writes=['qvT'])
                    else:
                        P.op('act', lambda e: e.copy(out=qvT[:, hp, :], in_=q_[:, :GT]), reads=[qk], writes=['qvT'])
                for t in range(TG):
                    for q4 in range(4):
                        s_, sk = sps.next()
                        for j in range(4):
                            hp = q4 * 4 + j
                            P.op('pe', lambda e, j=j, hp=hp: e.matmul(s_[:, j, :], lhsT=qvT[:, hp, t * 128:(t + 1) * 128],
                                                                      rhs=kT_bf[:, hp, :], start=True, stop=True),
                                 reads=['qvT', 'kT'], writes=[sk])
                        P.op('act', lambda e: e.copy(out=s_sb[t][:, q4 * 4:(q4 + 1) * 4, :], in_=s_[:]), reads=[sk], writes=[f's_sb{t}'])
                    for hp in range(16):
                        P.op('dve', lambda e: e.max(out=top[t][:, hp, 0:8], in_=s_sb[t][:, hp, :]), reads=[f's_sb{t}'], writes=[f'top{t}'])
                        P.op('dve', lambda e: e.match_replace(out=scr[:, 0:128], in_to_replace=top[t][:, hp, 0:8],
                                                              in_values=s_sb[t][:, hp, :], imm_value=NEG_BIG),
                             reads=[f's_sb{t}', f'top{t}'], writes=['scr'])
                        P.op('dve', lambda e: e.max(out=top[t][:, hp, 8:16], in_=scr[:, 0:128]), reads=['scr'], writes=[f'top{t}'])
                    for h in range(8):
                        P.op('dve', lambda e: e.tensor_tensor(
                            out=cand[:, h, :].rearrange("p (a b) -> p a b", a=16),
                            in0=top[t][:, 2 * h, :].unsqueeze(2).to_broadcast([128, 16, 16]),
                            in1=top[t][:, 2 * h + 1, :].unsqueeze(1).to_broadcast([128, 16, 16]), op=ALU.add),
                            reads=[f'top{t}'], writes=['cand'])
                    for h in range(8):
                        P.op('dve', lambda e: e.max(out=best[t][:, h, 0:8], in_=cand[:, h, :]), reads=['cand'], writes=[f'best{t}'])
                        P.op('dve', lambda e: e.match_replace(out=scr[:], in_to_replace=best[t][:, h, 0:8],
                                                              in_values=cand[:, h, :], imm_value=NEG_BIG),
                             reads=['cand', f'best{t}'], writes=['scr'])
                        P.op('dve', lambda e: e.max(out=best[t][:, h, 8:16], in_=scr[:]), reads=['scr'], writes=[f'best{t}'])
                    P.op('dve', lambda e: e.tensor_tensor(out=ez[:], in0=best[t][:], in1=best[t][:, :, 0:1].to_broadcast([128, 8, 16]),
                                                          op=ALU.subtract), reads=[f'best{t}'], writes=['ez'])
                    P.op('act', lambda e: e.activation(out=ez[:], in_=ez[:], func=AF.Exp), reads=['ez'], writes=['ez'])
                    P.op('dve', lambda e: e.tensor_reduce(out=zz[:], in_=ez[:], axis=AX.X, op=ALU.add), reads=['ez'], writes=['zz'])
                    P.op('act', lambda e: e.activation(out=zz[:], in_=zz[:], func=AF.Ln), reads=['zz'], writes=['zz'])
                    P.op('dve', lambda e: e.scalar_tensor_tensor(out=nb[t][:], in0=best[t][:, :, 0], scalar=-1.0, in1=zz[:],
                                                                 op0=ALU.mult, op1=ALU.subtract),
                         reads=[f'best{t}', 'zz'], writes=[f'nb{t}'])
                load_uv(0)
                for eb in range(NB):
                    if eb + 1 < NB:
                        load_uv(eb + 1)
                    U_, Uk = ubuf[eb % 2], f'Ub{eb % 2}'
                    V_, Vk = vbuf[eb % 2], f'Vb{eb % 2}'
                    for t in range(TG):
                        tsl = slice(g0 + t * 128, g0 + (t + 1) * 128)
                        for sub in range(EB // 512):
                            h_, hk = hps.next()
                            for kc in range(8):
                                P.op('pe', lambda e, kc=kc: e.matmul(h_[:], lhsT=x1T[:, kc, tsl], rhs=U_[:, kc, sub * 512:(sub + 1) * 512],
                                                                     start=(kc == 0), stop=(kc == 7)),
                                     reads=['x1T', Uk], writes=[hk])
                            P.op('act', lambda e: e.activation(out=gel[t][:, sub * 512:(sub + 1) * 512], in_=h_[:], func=AF.Gelu),
                                 reads=[hk], writes=[f'gel{t}'])
                        G_, Gk = Gb.next()
                        for h in range(8):
                            S_, Sk = Sb.next()
                            P.op('dve', lambda e: e.tensor_tensor(
                                out=S_[:],
                                in0=s_sb[t][:, 2 * h, eb * NI:(eb + 1) * NI].unsqueeze(2).to_broadcast([128, NI, 128]),
                                in1=s_sb[t][:, 2 * h + 1, :].unsqueeze(1).to_broadcast([128, NI, 128]), op=ALU.add),
                                reads=[f's_sb{t}'], writes=[Sk])
                            E_, Ek = Eb.next()
                            Sf = S_[:].rearrange("p a b -> p (a b)")
                            P.op('act', lambda e: e.activation(out=E_[:], in_=Sf, func=AF.Exp, bias=nb[t][:, h:h + 1], scale=1.0),
                                 reads=[Sk, f'nb{t}'], writes=[Ek])
                            if h == 0:
                                P.op('dve', lambda e: e.scalar_tensor_tensor(out=G_[:], in0=Sf, scalar=best[t][:, h, 15:16], in1=E_[:],
                                                                             op0=ALU.is_ge, op1=ALU.mult),
                                     reads=[Sk, Ek, f'best{t}'], writes=[Gk])
                            else:
                                W_, Wk = Wb.next()
                                P.op('dve', lambda e: e.scalar_tensor_tensor(out=W_[:], in0=Sf, scalar=best[t][:, h, 15:16], in1=E_[:],
                                                                             op0=ALU.is_ge, op1=ALU.mult),
                                     reads=[Sk, Ek, f'best{t}'], writes=[Wk])
                                P.op('pool', lambda e: e.tensor_tensor(out=G_[:], in0=G_[:], in1=W_[:], op=ALU.add),
                                     reads=[Wk, Gk], writes=[Gk])
                        GA_, GAk = GAb.next()
                        P.op('pool', lambda e: e.tensor_tensor(out=GA_[:], in0=G_[:], in1=gel[t][:], op=ALU.mult),
                             reads=[Gk, f'gel{t}'], writes=[GAk])
                        tp_, tpk = tpg.next()
                        for j in range(NI):
                            P.op('pe', lambda e, j=j: e.transpose(out=tp_[:, j, :], in_=GA_[:, j * 128:(j + 1) * 128], identity=identb[:]),
                                 reads=[GAk, 'identb'], writes=[tpk])
                        GT_, GTk = GATb.next()
                        P.op('act', lambda e: e.copy(out=GT_[:], in_=tp_[:]), reads=[tpk], writes=[GTk])
                        for half in range(2):
                            y_, ypk = yps.next()
                            for j in range(NI):
                                P.op('pe', lambda e, j=j: e.matmul(y_[:], lhsT=GT_[:, j, :], rhs=V_[:, j, half * 512:(half + 1) * 512],
                                                                   start=(j == 0), stop=(j == NI - 1)),
                                     reads=[GTk, Vk], writes=[ypk])
                            hs = slice(half * 512, (half + 1) * 512)
                            if eb == 0:
                                P.op('dve', lambda e: e.tensor_copy(out=yacc[t][:, hs], in_=y_[:]), reads=[ypk], writes=[f'yacc{t}'])
                            else:
                                P.op('dve', lambda e: e.tensor_tensor(out=yacc[t][:, hs], in0=yacc[t][:, hs], in1=y_[:], op=ALU.add),
                                     reads=[ypk, f'yacc{t}'], writes=[f'yacc{t}'])
                for t in range(TG):
                    tsl = slice(g0 + t * 128, g0 + (t + 1) * 128)
                    x_, xk = xr.next()
                    P.dma('sp', x_[:], x1d[tsl, :], writes=[xk], group=xk)
                    P.op('dve', lambda e: e.scalar_tensor_tensor(out=x_[:], in0=x_[:], scalar=DN_ALPHA, in1=yacc[t][:],
                                                                 op0=ALU.mult, op1=ALU.add),
                         reads=[xk, f'yacc{t}'], writes=[xk])
                    sc_, sck = sc.next()
                    emit_ln(P, x_, xk, gb, bb, epsb, sc_, sck)
                    P.dma('pool', out[tsl, :], x_[:], reads=[xk], group=xk + 'o')
            P.finish('sp')
    return nc


def mix_consts():
    c = {}
    j = np.arange(128)[:, None]
    s = np.arange(128)[None, :]
    c['tri'] = (j >= s).astype(np.float32)
    c['ones'] = np.ones((128, 128), np.float32)
    t = np.arange(512)[None, :]
    c['dmask'] = np.stack([((jj * 128 + np.arange(128)[:, None]) < t).astype(np.float32) for jj in range(4)], 0)
    c['ident'] = np.eye(128, dtype=np.float32)
    return c


def emit_sb(P, nc, S, a_qT, a_kT, a_v, o_a, cst):
    NQ = S // 512
    with contextlib.ExitStack() as st:
        P.stack = st
        tri = P.sb("tri", [128, 128], BF16)
        ones = P.sb("ones", [128, 128], BF16)
        dmask = P.sb("dmask", [128, 4, 512], BF16)
        P.dma('pool', tri[:], cst['tri'], writes=['tri'], group='const')
        P.dma('pool', ones[:], cst['ones'], writes=['ones'], group='const')
        P.dma('pool', dmask[:], cst['dmask'].rearrange("j p t -> p j t"), writes=['dmask'], group='const')
        qT = P.sb("sa_qT", [64, S], BF16)
        kT = P.sb("sa_kT", [64, S], BF16)
        kTn = P.sb("sa_kTn", [64, S], BF16)
        v = P.sb("sa_v", [128, S // 128, 64], BF16)
        for c0 in range(0, S, 2048):
            cw = min(2048, S - c0)
            P.dma('pool', qT[:, c0:c0 + cw], a_qT[:, c0:c0 + cw], writes=['a_qT'], group='a_ld')
            P.dma('pool', kT[:, c0:c0 + cw], a_kT[:, c0:c0 + cw], writes=['a_kT'], group='a_ld')
        P.dma('pool', v[:], a_v.rearrange("(n p) d -> p n d", p=128), writes=['a_v'], group='a_ld')
        P.op('act', lambda e: e.mul(out=kTn[:], in_=kT[:], mul=-0.125), reads=['a_kT'], writes=['a_kTn'])
        zps = Rot(P, "a_zp", [128, 512], F32, 2, psum=True)
        cps = Rot(P, "a_cp", [128, 512], F32, 2, psum=True)
        ops_ = Rot(P, "a_op", [64, 512], F32, 2, psum=True)
        eb = Rot(P, "a_e", [128, 512], F32, 2)
        spb = Rot(P, "a_sp", [128, 512], BF16, 3)
        wb = Rot(P, "a_w", [128, 512], BF16, 3)
        sacc = Rot(P, "a_sacc", [128, 512], BF16, 2)
        ob = Rot(P, "a_ob", [64, 512], F32, 2)
        for T in range(NQ):
            qs = slice(T * 512, (T + 1) * 512)
            o_, ok = ops_.next()
            sa, sak = sacc.next()
            kbs = list(range(4 * T + 3, -1, -1))
            for idx, kb in enumerate(kbs):
                ks = slice(kb * 128, (kb + 1) * 128)
                first = idx == 0
                last = idx == len(kbs) - 1
                diag = kb >= 4 * T
                z_, zk = zps.next()
                P.op('pe', lambda e: e.matmul(z_[:], lhsT=kT[:, ks], rhs=qT[:, qs], start=True, stop=True),
                     reads=['a_kT', 'a_qT'], writes=[zk])
                e_, ek = eb.next()
                P.op('act', lambda e: e.activation(out=e_[:], in_=z_[:], func=AF.Exp, scale=0.125), reads=[zk], writes=[ek])
                sp, spk = spb.next()
                P.op('act', lambda e: e.activation(out=sp[:], in_=e_[:], func=AF.Ln, bias=1.0, scale=1.0), reads=[ek], writes=[spk])
                if diag:
                    P.op('pool', lambda e: e.tensor_tensor(out=sp[:], in0=sp[:], in1=dmask[:, kb - 4 * T, :], op=ALU.mult),
                         reads=[spk, 'dmask'], writes=[spk])
                c_, ck = cps.next()
                P.op('pe', lambda e: e.matmul(c_[:], lhsT=tri[:], rhs=sp[:], start=True, stop=False), reads=['tri', spk], writes=[ck])
                if not first:
                    P.op('pe', lambda e: e.matmul(c_[:], lhsT=ones[:], rhs=sa[:], start=False, stop=False), reads=['ones', sak], writes=[ck])
                P.op('pe', lambda e: e.matmul(c_[:], lhsT=kTn[:, ks], rhs=qT[:, qs], start=False, stop=True),
                     reads=['a_kTn', 'a_qT'], writes=[ck])
                w_, wk = wb.next()
                P.op('act', lambda e: e.activation(out=w_[:], in_=c_[:], func=AF.Exp, scale=-1.0), reads=[ck], writes=[wk])
                if diag:
                    P.op('dve', lambda e: e.tensor_tensor(out=w_[:], in0=w_[:], in1=dmask[:, kb - 4 * T, :], op=ALU.mult),
                         reads=[wk, 'dmask'], writes=[wk])
                P.op('pe', lambda e: e.matmul(o_[:], lhsT=v[:, kb, :], rhs=w_[:], start=first, stop=last), reads=['a_v', wk], writes=[ok])
                if not last:
                    if first:
                        P.op('pool', lambda e: e.tensor_copy(out=sa[:], in_=sp[:]), reads=[spk], writes=[sak])
                    else:
                        P.op('pool', lambda e: e.tensor_tensor(out=sa[:], in0=sa[:], in1=sp[:], op=ALU.add), reads=[spk, sak], writes=[sak])
            ob_, obk = ob.next()
            P.op('dve', lambda e: e.tensor_copy(out=ob_[:], in_=o_[:]), reads=[ok], writes=[obk])
            P.dma('sp', o_a[:, qs], ob_[:], reads=[obk], group=obk)
        barrier(P)


def head_consts(h):
    c = {}
    gam = 1.0 - 2.0 ** (-5.0 - h)
    lg = np.log1p(-(2.0 ** (-5.0 - h)))
    i = np.arange(128)
    diff = i[None, :] - i[:, None]
    c['r_decT'] = np.where(diff >= 0, 0.125 * np.exp(lg * np.maximum(diff, 0)), 0.0).astype(np.float32)
    c['r_qdec'] = np.tile(np.exp(lg * (i + 1.0))[None, :], (64, 1)).astype(np.float32)
    c['r_kdec'] = (0.125 * np.exp(lg * (127.0 - i)))[:, None].astype(np.float32)
    c['r_cdec'] = np.full((64, 1), np.exp(lg * 128.0), np.float32)
    return c


def emit_headnorm(P, o8, o8k, n, center, g_t, b_t, gate_d, out_d, epsb, bufs, tag):
    oc, ock = bufs['oc'].next()
    gt, gtk = bufs['gt'].next()
    sm, smk = bufs['sm'].next()
    P.dma('sp', gt[:, :n, :], gate_d, writes=[gtk], group=gtk)
    P.op('act', lambda e: e.copy(out=oc[:, :n, :], in_=o8), reads=[o8k], writes=[ock])
    P.op('act', lambda e: e.activation(out=gt[:, :n, :], in_=gt[:, :n, :], func=AF.Silu), reads=[gtk], writes=[gtk])
    if center:
        P.op('dve', lambda e: e.tensor_reduce(out=sm[:, 0:n], in_=oc[:, :n, :], axis=AX.X, op=ALU.add), reads=[ock], writes=[smk + 'a'])
        P.op('dve', lambda e: e.tensor_scalar(out=sm[:, 0:n], in0=sm[:, 0:n], scalar1=-1.0 / 64.0, scalar2=None, op0=ALU.mult),
             reads=[smk + 'a'], writes=[smk + 'a'])
        P.op('dve', lambda e: e.tensor_tensor(out=oc[:, :n, :], in0=oc[:, :n, :],
                                              in1=sm[:, 0:n].unsqueeze(2).to_broadcast([128, n, 64]), op=ALU.add),
             reads=[ock, smk + 'a'], writes=[ock])
    sq, sqk = bufs['sq'].next()
    P.op('pool', lambda e: e.tensor_tensor(out=sq[:, :n, :], in0=oc[:, :n, :], in1=oc[:, :n, :], op=ALU.mult), reads=[ock], writes=[sqk])
    P.op('dve', lambda e: e.tensor_reduce(out=sm[:, 8:8 + n], in_=sq[:, :n, :], axis=AX.X, op=ALU.add), reads=[sqk], writes=[smk + 'b'])
    P.op('act', lambda e: e.activation(out=sm[:, 16:16 + n], in_=sm[:, 8:8 + n], func=AF.Sqrt, bias=epsb[:, 0:1], scale=1.0 / 64.0),
         reads=[smk + 'b', 'epsb'], writes=[smk + 'c'])
    P.op('dve', lambda e: e.reciprocal(out=sm[:, 24:24 + n], in_=sm[:, 16:16 + n]), reads=[smk + 'c'], writes=[smk + 'd'])
    P.op('dve', lambda e: e.tensor_tensor(out=oc[:, :n, :], in0=oc[:, :n, :],
                                          in1=sm[:, 24:24 + n].unsqueeze(2).to_broadcast([128, n, 64]), op=ALU.mult),
         reads=[ock, smk + 'd'], writes=[ock])
    P.op('pool', lambda e: e.tensor_tensor(out=oc[:, :n, :], in0=oc[:, :n, :],
                                           in1=g_t[:].unsqueeze(1).to_broadcast([128, n, 64]), op=ALU.mult),
         reads=[ock, tag + 'g'], writes=[ock])
    if b_t is not None:
        P.op('pool', lambda e: e.tensor_tensor(out=oc[:, :n, :], in0=oc[:, :n, :],
                                               in1=b_t[:].unsqueeze(1).to_broadcast([128, n, 64]), op=ALU.add),
             reads=[ock, tag + 'b'], writes=[ock])
    P.op('dve', lambda e: e.tensor_tensor(out=oc[:, :n, :], in0=oc[:, :n, :], in1=gt[:, :n, :], op=ALU.mult),
         reads=[ock, gtk], writes=[ock])
    P.dma('sp', out_d, oc[:, :n, :], reads=[ock], group=ock + 'o')


def norm_bufs(P, tag):
    return dict(oc=Rot(P, tag + "oc", [128, 8, 64], F32, 2), gt=Rot(P, tag + "gt", [128, 8, 64], F32, 2),
                sq=Rot(P, tag + "sq", [128, 8, 64], F32, 2), sm=Rot(P, tag + "sm", [128, 32], F32, 2))


def emit_ret(P, nc, S, c_qT, c_kT, c_k, c_v, c_g, gn_g, gn_b, o_c, hc, epsb):
    NC = S // 128
    with contextlib.ExitStack() as st:
        P.stack = st
        decT = P.sb("r_decT", [128, 128], F32)
        qdec = P.sb("r_qdec", [64, 128], F32)
        kdec = P.sb("r_kdec", [128, 1], F32)
        cdec = P.sb("r_cdec", [64, 1], F32)
        P.dma('sp', decT[:], hc['r_decT'], writes=['r_decT'], group='const')
        P.dma('sp', qdec[:], hc['r_qdec'], writes=['r_qdec'], group='const')
        P.dma('sp', kdec[:], hc['r_kdec'], writes=['r_kdec'], group='const')
        P.dma('sp', cdec[:], hc['r_cdec'], writes=['r_cdec'], group='const')
        gg = P.sb("r_gg", [128, 64], F32)
        gbt = P.sb("r_gb", [128, 64], F32)
        P.dma('sp', gg[:], gn_g.partition_broadcast(128), writes=['rg'], group='const')
        P.dma('sp', gbt[:], gn_b.partition_broadcast(128), writes=['rb'], group='const')
        qT = P.sb("r_qT", [64, NC, 128], BF16)
        qdT = P.sb("r_qdT", [64, NC, 128], BF16)
        kT = P.sb("r_kT", [64, NC, 128], BF16)
        v = P.sb("r_v", [128, NC, 64], BF16)
        kt = P.sb("r_kt", [128, NC, 64], F32)
        kd = P.sb("r_kd", [128, NC, 64], BF16)
        P.dma('pool', qT[:], c_qT.rearrange("d (c i) -> d c i", i=128), writes=['r_qT'], group='r_ld')
        P.dma('pool', kT[:], c_kT.rearrange("d (c i) -> d c i", i=128), writes=['r_kT'], group='r_ld')
        P.dma('pool', v[:], c_v.rearrange("(c p) d -> p c d", p=128), writes=['r_v'], group='r_ld')
        P.dma('sp', kt[:], c_k.rearrange("(c p) d -> p c d", p=128), writes=['r_kt'], group='r_ld2')
        P.op('dve', lambda e: e.tensor_tensor(out=qdT[:], in0=qT[:], in1=qdec[:].unsqueeze(1).to_broadcast([64, NC, 128]), op=ALU.mult),
             reads=['r_qT', 'r_qdec'], writes=['r_qdT'])
        P.op('dve', lambda e: e.tensor_scalar(out=kd[:], in0=kt[:], scalar1=kdec[:, 0:1], scalar2=None, op0=ALU.mult),
             reads=['r_kt', 'r_kdec'], writes=['r_kd'])
        R32 = P.sb("r_R32", [64, 64], F32)
        Rb = P.sb("r_Rb", [64, 64], BF16)
        P.op('pool', lambda e: e.memset(R32[:], 0.0), writes=['r_R32'])
        P.op('pool', lambda e: e.memset(Rb[:], 0.0), writes=['r_Rb'])
        aps = Rot(P, "r_ap", [128, 128], F32, 2, psum=True)
        ops_ = Rot(P, "r_op", [128, 8, 64], F32, 2, psum=True)
        dps = Rot(P, "r_dp", [64, 64], F32, 2, psum=True)
        am = Rot(P, "r_am", [128, 128], BF16, 3)
        nb_ = norm_bufs(P, "r_")
        o8, o8k = None, None
        for c in range(NC):
            j = c % 8
            if j == 0:
                o8, o8k = ops_.next()
            a_, ak = aps.next()
            P.op('pe', lambda e: e.matmul(a_[:], lhsT=kT[:, c, :], rhs=qT[:, c, :], start=True, stop=True),
                 reads=['r_kT', 'r_qT'], writes=[ak])
            m_, mk = am.next()
            P.op('dve', lambda e: e.tensor_tensor(out=m_[:], in0=a_[:], in1=decT[:], op=ALU.mult), reads=[ak, 'r_decT'], writes=[mk])
            P.op('pe', lambda e: e.matmul(o8[:, j, :], lhsT=m_[:], rhs=v[:, c, :], start=True, stop=False), reads=[mk, 'r_v'], writes=[o8k])
            P.op('pe', lambda e: e.matmul(o8[:, j, :], lhsT=qdT[:, c, :], rhs=Rb[:], start=False, stop=True), reads=['r_qdT', 'r_Rb'], writes=[o8k])
            d_, dk = dps.next()
            P.op('pe', lambda e: e.matmul(d_[:], lhsT=kd[:, c, :], rhs=v[:, c, :], start=True, stop=True), reads=['r_kd', 'r_v'], writes=[dk])
            P.op('dve', lambda e: e.scalar_tensor_tensor(out=R32[:], in0=R32[:], scalar=cdec[:, 0:1], in1=d_[:], op0=ALU.mult, op1=ALU.add),
                 reads=[dk, 'r_R32', 'r_cdec'], writes=['r_R32'])
            P.op('act', lambda e: e.copy(out=Rb[:], in_=R32[:]), reads=['r_R32'], writes=['r_Rb'])
            if j == 7 or c == NC - 1:
                n = j + 1
                c0 = c - j
                emit_headnorm(P, o8[:, :n, :], o8k, n, True, gg, gbt,
                              c_g[c0 * 128:(c0 + n) * 128, :].rearrange("(c p) d -> p c d", p=128),
                              o_c[c0 * 128:(c0 + n) * 128, :].rearrange("(c p) d -> p c d", p=128), epsb, nb_, 'r')
        barrier(P)


def gla_consts():
    c = {}
    i = np.arange(128)
    c['g_cum'] = (-(1.0 / 16.0) * (i[:, None] <= i[None, :])).astype(np.float32)
    c['g_suf'] = (-(1.0 / 16.0) * (i[:, None] > i[None, :])).astype(np.float32)
    c['g_caus'] = (i[:, None] <= i[None, :]).astype(np.float32)
    return c


def emit_gla(P, nc, S, d_qT, d_kT, d_k, d_v, d_r, d_lrT, w_al, b_al, norm_g, o_d, gc, epsb):
    NC = S // 128
    LNSC = float(np.log(32.0 ** -0.5))
    with contextlib.ExitStack() as st:
        P.stack = st
        cum = P.sb("g_cum", [128, 128], F32)
        suf = P.sb("g_suf", [128, 128], F32)
        caus = P.sb("g_caus", [128, 128], F32)
        P.dma('sp', cum[:], gc['g_cum'], writes=['g_cum'], group='const')
        P.dma('sp', suf[:], gc['g_suf'], writes=['g_suf'], group='const')
        P.dma('sp', caus[:], gc['g_caus'], writes=['g_caus'], group='const')
        ng = P.sb("g_ng", [128, 64], F32)
        P.dma('sp', ng[:], norm_g.partition_broadcast(128), writes=['gg'], group='const')
        lrT = P.sb("g_lrT", [17, S], F32)
        wa = P.sb("g_wa", [17, 32], F32)
        P.op('pool', lambda e: e.memset(lrT[:], 1.0), writes=['g_lrT'])
        P.dma('sp', lrT[0:16, :], d_lrT, writes=['g_lrT'], group='g_ld')
        P.dma('sp', wa[0:16, :], w_al, writes=['g_wa'], group='g_ld')
        P.dma('sp', wa[16:17, :], b_al.unsqueeze(0), writes=['g_wa'], group='g_ld')
        qT = P.sb("g_qT", [32, NC, 128], BF16)
        kT = P.sb("g_kT", [32, NC, 128], BF16)
        qeT = P.sb("g_qeT", [32, NC, 128], BF16)
        keT = P.sb("g_keT", [32, NC, 128], BF16)
        kt = P.sb("g_kt", [128, NC, 32], F32)
        kl = P.sb("g_kl", [128, NC, 32], BF16)
        v = P.sb("g_v", [128, NC, 64], BF16)
        A = P.sb("g_A", [128, NC, 32], F32)
        ebl = P.sb("g_ebl", [32, NC], F32)
        P.dma('pool', qT[:], d_qT.rearrange("d (c i) -> d c i", i=128), writes=['g_qT'], group='g_ld3')
        P.dma('pool', kT[:], d_kT.rearrange("d (c i) -> d c i", i=128), writes=['g_kT'], group='g_ld3')
        P.dma('pool', v[:], d_v.rearrange("(c p) d -> p c d", p=128), writes=['g_v'], group='g_ld3')
        P.dma('sp', kt[:], d_k.rearrange("(c p) d -> p c d", p=128), writes=['g_kt'], group='g_ld2')
        xps = Rot(P, "g_xp", [128, 16, 32], F32, 1, psum=True)
        bps = Rot(P, "g_bp", [32, 4, 128], F32, 1, psum=True)
        aps = Rot(P, "g_ap", [128, 128], F32, 2, psum=True)
        ops_ = Rot(P, "g_op", [128, 8, 64], F32, 2, psum=True)
        dps = Rot(P, "g_dp", [32, 64], F32, 2, psum=True)
        et = Rot(P, "g_et", [128, 16, 32], F32, 2)
        ebT = Rot(P, "g_ebT", [32, 4, 128], F32, 2)
        enT = Rot(P, "g_enT", [32, 4, 128], F32, 2)
        for c0 in range(0, NC, 16):
            n = min(16, NC - c0)
            x_, xk = xps.next()
            for j in range(n):
                P.op('pe', lambda e, j=j: e.matmul(x_[:, j, :], lhsT=lrT[:, (c0 + j) * 128:(c0 + j + 1) * 128], rhs=wa[:], start=True, stop=True),
                     reads=['g_lrT', 'g_wa'], writes=[xk])
            e_, ek = et.next()
            P.op('act', lambda e: e.activation(out=e_[:, :n, :], in_=x_[:, :n, :], func=AF.Exp, scale=-1.0), reads=[xk], writes=[ek])
            P.op('act', lambda e: e.activation(out=A[:, c0:c0 + n, :], in_=e_[:, :n, :], func=AF.Ln, bias=1.0, scale=1.0), reads=[ek], writes=['g_A'])
        for c0 in range(0, NC, 16):
            n = min(16, NC - c0)
            x_, xk = xps.next()
            for j in range(n):
                P.op('pe', lambda e, j=j: e.matmul(x_[:, j, :], lhsT=suf[:], rhs=A[:, c0 + j, :], start=True, stop=True),
                     reads=['g_suf', 'g_A'], writes=[xk])
            e_, ek = et.next()
            P.op('act', lambda e: e.activation(out=e_[:, :n, :], in_=x_[:, :n, :], func=AF.Exp), reads=[xk], writes=[ek])
            P.op('dve', lambda e: e.tensor_tensor(out=kl[:, c0:c0 + n, :], in0=kt[:, c0:c0 + n, :], in1=e_[:, :n, :], op=ALU.mult),
                 reads=[ek, 'g_kt'], writes=['g_kl'])
        for c0 in range(0, NC, 4):
            n = min(4, NC - c0)
            b_, bk = bps.next()
            for j in range(n):
                P.op('pe', lambda e, j=j: e.matmul(b_[:, j, :], lhsT=A[:, c0 + j, :], rhs=cum[:], start=True, stop=True),
                     reads=['g_A', 'g_cum'], writes=[bk])
            eb_, ebk = ebT.next()
            en_, enk = enT.next()
            P.op('act', lambda e: e.activation(out=eb_[:, :n, :], in_=b_[:, :n, :], func=AF.Exp, bias=LNSC, scale=1.0), reads=[bk], writes=[ebk])
            P.op('act', lambda e: e.activation(out=en_[:, :n, :], in_=b_[:, :n, :], func=AF.Exp, scale=-1.0), reads=[bk], writes=[enk])
            P.op('act', lambda e: e.activation(out=ebl[:, c0:c0 + n], in_=b_[:, :n, 127], func=AF.Exp), reads=[bk], writes=['g_ebl'])
            P.op('dve', lambda e: e.tensor_tensor(out=qeT[:, c0:c0 + n, :], in0=qT[:, c0:c0 + n, :], in1=eb_[:, :n, :], op=ALU.mult),
                 reads=['g_qT', ebk], writes=['g_qeT'])
            P.op('dve', lambda e: e.tensor_tensor(out=keT[:, c0:c0 + n, :], in0=kT[:, c0:c0 + n, :], in1=en_[:, :n, :], op=ALU.mult),
                 reads=['g_kT', enk], writes=['g_keT'])
        St32 = P.sb("g_St32", [32, 64], F32)
        Stb = P.sb("g_Stb", [32, 64], BF16)
        P.op('pool', lambda e: e.memset(St32[:], 0.0), writes=['g_St32'])
        P.op('pool', lambda e: e.memset(Stb[:], 0.0), writes=['g_Stb'])
        am = Rot(P, "g_am", [128, 128], BF16, 3)
        nb_ = norm_bufs(P, "g_")
        o8, o8k = None, None
        for c in range(NC):
            j = c % 8
            if j == 0:
                o8, o8k = ops_.next()
            a_, ak = aps.next()
            P.op('pe', lambda e: e.matmul(a_[:], lhsT=keT[:, c, :], rhs=qeT[:, c, :], start=True, stop=True), reads=['g_keT', 'g_qeT'], writes=[ak])
            m_, mk = am.next()
            P.op('dve', lambda e: e.tensor_tensor(out=m_[:], in0=a_[:], in1=caus[:], op=ALU.mult), reads=[ak, 'g_caus'], writes=[mk])
            P.op('pe', lambda e: e.matmul(o8[:, j, :], lhsT=m_[:], rhs=v[:, c, :], start=True, stop=False), reads=[mk, 'g_v'], writes=[o8k])
            P.op('pe', lambda e: e.matmul(o8[:, j, :], lhsT=qeT[:, c, :], rhs=Stb[:], start=False, stop=True), reads=['g_qeT', 'g_Stb'], writes=[o8k])
            d_, dk = dps.next()
            P.op('pe', lambda e: e.matmul(d_[:], lhsT=kl[:, c, :], rhs=v[:, c, :], start=True, stop=True), reads=['g_kl', 'g_v'], writes=[dk])
            P.op('dve', lambda e: e.scalar_tensor_tensor(out=St32[:], in0=St32[:], scalar=ebl[:, c:c + 1], in1=d_[:], op0=ALU.mult, op1=ALU.add),
                 reads=[dk, 'g_St32', 'g_ebl'], writes=['g_St32'])
            P.op('act', lambda e: e.copy(out=Stb[:], in_=St32[:]), reads=['g_St32'], writes=['g_Stb'])
            if j == 7 or c == NC - 1:
                n = j + 1
                c0 = c - j
                emit_headnorm(P, o8[:, :n, :], o8k, n, False, ng, None,
                              d_r[c0 * 128:(c0 + n) * 128, :].rearrange("(c p) d -> p c d", p=128),
                              o_d[c0 * 128:(c0 + n) * 128, :].rearrange("(c p) d -> p c d", p=128), epsb, nb_, 'g')
        barrier(P)


def nsa_dims(S):
    Nc = (S - 32) // 16 + 1
    NJ = (Nc + 127) // 128
    return Nc, NJ, NJ * 128, S // 64


def nsa_consts(S, h):
    Nc, NJ, NcP, NSEL = nsa_dims(S)
    c = {}
    t = np.arange(S)
    c['n_qaug'] = np.stack([t // 64, t % 64, np.ones(S), np.ones(S)], 0).astype(np.float32)
    heads = [0, 1, 2, 3, h]
    slopes = [2.0 ** (-2.0 * (g + 1)) for g in heads]
    c['n_kpos'] = np.stack([np.stack([np.full(S, -512.0 * sl), np.full(S, -8.0 * sl), 512.0 * sl * (t // 64), 8.0 * sl * (t % 64)], 0)
                            for sl in slopes], 0).astype(np.float32)
    e = 16 * np.arange(NcP) + 31
    c['n_kcmp'] = np.stack([np.stack([np.full(NcP, -512.0 * sl), np.full(NcP, -8.0 * sl), 512.0 * sl * (e // 64), 8.0 * sl * (e % 64)], 0)
                            for sl in slopes], 0).astype(np.float32)
    cc = np.arange(NcP)
    n = np.arange(NSEL)
    ovl = ((16 * cc[:, None] < 64 * n[None, :] + 64) & (16 * cc[:, None] + 32 > 64 * n[None, :]) & (cc[:, None] < Nc)).astype(np.float32)
    c['n_ovl'] = np.concatenate([ovl, np.ones((NcP, 1), np.float32)], 1)
    cl = np.arange(128)[:, None]
    tl = np.arange(512)[None, :]
    c['n_cmask'] = np.stack([np.where(d * 512 + tl - 16 * cl - 31 >= 0, 0.0, NEG_BIG) for d in range(5)], 0).astype(np.float32)
    tq = np.arange(128)[:, None]
    sk = np.arange(512)[None, :]
    c['n_smask'] = np.stack([np.where(sk <= 128 * r + tq, 0.0, NEG_BIG) for r in range(4)], 0).astype(np.float32)
    sl_ = np.arange(128)[:, None]
    tl_ = np.arange(128)[None, :]
    c['n_wm0'] = np.where(tl_ >= sl_, 0.0, NEG_BIG).astype(np.float32)
    c['n_wm4'] = np.where(tl_ < sl_, 0.0, NEG_BIG).astype(np.float32)
    return c


def emit_nsa(P, nc, S, d, o_b, cst, identb):
    Nc, NJ, NcP, NSEL = nsa_dims(S)
    NQ = S // 512
    NB = S // 128
    with contextlib.ExitStack() as st:
        P.stack = st
        kcaug = P.sb("n_kcaug", [68, 5, NcP], BF16)
        vcaug = P.sb("n_vcaug", [128, NJ, 65], BF16)
        ovl = P.sb("n_ovl", [128, NJ, NSEL + 1], BF16)
        ksaug = P.sb("n_ksaug", [68, S], BF16)
        kwaug = P.sb("n_kwaug", [68, S], BF16)
        vsaug = P.sb("n_vsaug", [128, NB, 65], BF16)
        vwaug = P.sb("n_vwaug", [128, NB, 65], BF16)
        cmask = P.sb("n_cmask", [128, 5, 512], F32)
        smask = P.sb("n_smask", [128, 4, 512], F32)
        wm0 = P.sb("n_wm0", [128, 128], F32)
        wm4 = P.sb("n_wm4", [128, 128], F32)
        gate = P.sb("n_gate", [128, NB, 3], F32)
        P.dma('sp', cmask[:], cst['n_cmask'].rearrange("j p t -> p j t"), writes=['n_cmask'], group='const')
        P.dma('sp', smask[:], cst['n_smask'].rearrange("j p t -> p j t"), writes=['n_smask'], group='const')
        P.dma('sp', wm0[:], cst['n_wm0'], writes=['n_wm0'], group='const')
        P.dma('sp', wm4[:], cst['n_wm4'], writes=['n_wm4'], group='const')
        P.dma('sp', gate[:], d['b_g'].rearrange("(c p) k -> p c k", p=128), writes=['n_gate'], group='const')
        P.op('act', lambda e: e.activation(out=gate[:], in_=gate[:], func=AF.Sigmoid), reads=['n_gate'], writes=['n_gate'])
        P.dma('pool', ovl[:], cst['n_ovl'].rearrange("(j p) n -> p j n", p=128), writes=['n_ovl'], group='n_ld')
        for c0 in range(0, S, 2048):
            cw = min(2048, S - c0)
            P.dma('pool', ksaug[0:64, c0:c0 + cw], d['b_ksT'][:, c0:c0 + cw], writes=['n_ksaug'], group='n_ld')
            P.dma('pool', kwaug[0:64, c0:c0 + cw], d['b_kwT'][:, c0:c0 + cw], writes=['n_kwaug'], group='n_ld')
            P.dma('pool', ksaug[64:68, c0:c0 + cw], cst['n_kpos'][4, :, c0:c0 + cw], writes=['n_ksaug'], group='n_ld')
            P.dma('pool', kwaug[64:68, c0:c0 + cw], cst['n_kpos'][4, :, c0:c0 + cw], writes=['n_kwaug'], group='n_ld')
        P.op('dve', lambda e: e.memset(vsaug[:], 1.0), writes=['n_vsaug'])
        P.op('dve', lambda e: e.memset(vwaug[:], 1.0), writes=['n_vwaug'])
        P.op('dve', lambda e: e.memset(vcaug[:], 1.0), writes=['n_vcaug'])
        P.dma('pool', vsaug[:, :, 0:64], d['b_vs'].rearrange("(c p) d -> p c d", p=128), writes=['n_vsaug'], group='n_ld')
        P.dma('pool', vwaug[:, :, 0:64], d['b_vw'].rearrange("(c p) d -> p c d", p=128), writes=['n_vwaug'], group='n_ld')
        for g in range(5):
            P.dma('pool', kcaug[64:68, g, :], cst['n_kcmp'][g], writes=['n_kcaug'], group='n_ld')
        with contextlib.ExitStack() as st2:
            P.stack = st2
            kcT = P.sb("n_kcT", [64, S], BF16)
            vcT = P.sb("n_vcT", [64, S], BF16)
            wk = P.sb("n_wk", [64, 32, 64], BF16)
            wv = P.sb("n_wv", [64, 32, 64], BF16)
            posT = P.sb("n_posT", [64, 32], BF16)
            posbc = P.sb("n_posbc", [64, 32, 128], BF16)
            pbs = P.sb("n_pbs", [64, 1], F32)
            for c0 in range(0, S, 2048):
                cw = min(2048, S - c0)
                P.dma('pool', kcT[:, c0:c0 + cw], d['b_kcT'][:, c0:c0 + cw], writes=['n_kcT'], group='n_ld2')
                P.dma('pool', vcT[:, c0:c0 + cw], d['b_vcT'][:, c0:c0 + cw], writes=['n_vcT'], group='n_ld2')
            P.dma('pool', wk[:], d['cmp_wk'].rearrange("(l d) o -> d l o", d=64), writes=['n_wk'], group='n_ld2')
            P.dma('pool', wv[:], d['cmp_wv'].rearrange("(l d) o -> d l o", d=64), writes=['n_wv'], group='n_ld2')
            P.dma('pool', posT[:], d['cmp_posT'], writes=['n_posT'], group='n_ld2')
            P.op('dve', lambda e: e.tensor_copy(out=posbc[:], in_=posT[:].unsqueeze(2).to_broadcast([64, 32, 128])),
                 reads=['n_posT'], writes=['n_posbc'])
            kps = P.ps("n_kps", [64, 512], F32)
            pps = P.ps("n_pps", [64, 2], F32)
            vps = Rot(P, "n_vps", [128, 64], F32, 2, psum=True)
            for l in range(32):
                P.op('pe', lambda e: e.matmul(kps[:, 0:Nc], lhsT=wk[:, l, :], rhs=kcT[:, l:l + 16 * (Nc - 1) + 1:16], start=(l == 0), stop=(l == 31)),
                     reads=['n_wk', 'n_kcT'], writes=['n_kps'])
            for l in range(32):
                P.op('pe', lambda e: e.matmul(pps[:, 0:2], lhsT=wk[:, l, :], rhs=posT[:, l:l + 1].to_broadcast([64, 2]), start=(l == 0), stop=(l == 31)),
                     reads=['n_wk', 'n_posT'], writes=['n_pps'])
            P.op('dve', lambda e: e.tensor_copy(out=pbs[:], in_=pps[:, 0:1]), reads=['n_pps'], writes=['n_pbs'])
            P.op('dve', lambda e: e.memset(kcaug[0:64, :, :], 0.0), writes=['n_kcaug'])
            for g in range(5):
                P.op('dve', lambda e: e.tensor_scalar(out=kcaug[0:64, g, 0:Nc], in0=kps[:, 0:Nc], scalar1=pbs[:, 0:1], scalar2=None, op0=ALU.add),
                     reads=['n_kps', 'n_pbs'], writes=['n_kcaug'])
            for j in range(NJ):
                cw = min(128, Nc - j * 128)
                v_, vk = vps.next()
                for l in range(32):
                    c0 = j * 128 * 16 + l
                    P.op('pe', lambda e: e.matmul(v_[0:cw, :], lhsT=vcT[:, c0:c0 + 16 * (cw - 1) + 1:16], rhs=wv[:, l, :], start=(l == 0), stop=False),
                         reads=['n_vcT', 'n_wv'], writes=[vk])
                for l in range(32):
                    P.op('pe', lambda e: e.matmul(v_[0:cw, :], lhsT=posbc[:, l, 0:cw], rhs=wv[:, l, :], start=False, stop=(l == 31)),
                         reads=['n_posbc', 'n_wv'], writes=[vk])
                P.op('act', lambda e: e.copy(out=vcaug[0:cw, j, 0:64], in_=v_[0:cw, :]), reads=[vk], writes=['n_vcaug'])
            barrier(P)
        P.stack = st
        qa = [Rot(P, f"n_qa{g}", [68, 512], BF16, 2) for g in range(5)]
        zps = Rot(P, "n_zp", [128, 512], F32, 2, psum=True)
        ups = Rot(P, "n_up", [128, max(NSEL + 1, 65)], F32, 2, psum=True)
        tps = Rot(P, "n_tp", [128, 4, 128], BF16, 2, psum=True)
        osp = P.ps("n_osp", [128, 65], F32)
        owp = P.ps("n_owp", [128, 65], F32)
        zm = Rot(P, "n_zm", [128, 512], F32, 3)
        ex = [Rot(P, f"n_ex{j}", [128, 512], BF16, 2) for j in range(NJ)]
        imp = Rot(P, "n_imp", [128, 4, NSEL], F32, 2)
        ocm = Rot(P, "n_ocm", [128, 4, 65], F32, 2)
        rz = Rot(P, "n_rz", [128, 4], F32, 4)
        sel = Rot(P, "n_sel", [128, NSEL], F32, 2)
        t8 = Rot(P, "n_t8", [128, 16], F32, 2)
        scr = Rot(P, "n_scr", [128, NSEL], F32, 2)
        pb = Rot(P, "n_pb", [128, 512], BF16, 3)
        pT = Rot(P, "n_pT", [128, 4, 128], BF16, 3)
        wz = Rot(P, "n_wz", [128, 128], F32, 3)
        we = Rot(P, "n_we", [128, 128], BF16, 3)
        ob = Rot(P, "n_ob", [128, 64], F32, 2)
        wq3 = Rot(P, "n_w3", [128, 8], F32, 2)
        fneg = nc.gpsimd.to_reg(NEG_BIG)
        fpos = nc.gpsimd.to_reg(-NEG_BIG)
        for T in range(NQ):
            qs = slice(T * 512, (T + 1) * 512)
            qt = []
            for g in range(5):
                q_, qk = qa[g].next()
                src = d['b_qTo'] if g == 4 else d['b_qT4'][g]
                P.dma('pool', q_[0:64, :], src[:, qs], writes=[qk], group=qk)
                P.dma('pool', q_[64:68, :], cst['n_qaug'][:, qs], writes=[qk], group=qk)
                qt.append((q_, qk))
            jmax = min(NJ - 1, (32 * T + 30) // 128)
            im_, imk = imp.next()
            oc_, ock = ocm.next()
            for g in range(5):
                q_, qk = qt[g]
                exs = []
                for j in range(jmax + 1):
                    z_, zk = zps.next()
                    P.op('pe', lambda e: e.matmul(z_[:], lhsT=kcaug[:, g, j * 128:(j + 1) * 128], rhs=q_[:], start=True, stop=True),
                         reads=['n_kcaug', qk], writes=[zk])
                    dl = T - 4 * j
                    e_, ek = ex[j].next()
                    if dl <= 4:
                        m_, mk = zm.next()
                        P.op('dve', lambda e: e.tensor_tensor(out=m_[:], in0=z_[:], in1=cmask[:, dl, :], op=ALU.add), reads=[zk, 'n_cmask'], writes=[mk])
                        P.op('act', lambda e: e.activation(out=e_[:], in_=m_[:], func=AF.Exp, scale=0.125), reads=[mk], writes=[ek])
                    else:
                        P.op('act', lambda e: e.activation(out=e_[:], in_=z_[:], func=AF.Exp, scale=0.125), reads=[zk], writes=[ek])
                    exs.append((e_, ek))
                for qb in range(4):
                    u_, uk = ups.next()
                    if g < 4:
                        for j, (e_, ek) in enumerate(exs):
                            P.op('pe', lambda e: e.matmul(u_[:, 0:NSEL + 1], lhsT=e_[:, qb * 128:(qb + 1) * 128], rhs=ovl[:, j, :], start=(j == 0), stop=(j == jmax)),
                                 reads=[ek, 'n_ovl'], writes=[uk])
                        r_, rk = rz.next()
                        P.op('dve', lambda e: e.tensor_scalar(out=r_[:, 0:1], in0=u_[:, NSEL:NSEL + 1], scalar1=1e-30, scalar2=None, op0=ALU.max),
                             reads=[uk], writes=[rk])
                        P.op('dve', lambda e: e.reciprocal(out=r_[:, 1:2], in_=r_[:, 0:1]), reads=[rk], writes=[rk + 'r'])
                        if g == 0:
                            P.op('dve', lambda e: e.tensor_scalar(out=im_[:, qb, :], in0=u_[:, 0:NSEL], scalar1=r_[:, 1:2], scalar2=None, op0=ALU.mult),
                                 reads=[uk, rk + 'r'], writes=[imk + str(qb)])
                        else:
                            P.op('dve', lambda e: e.scalar_tensor_tensor(out=im_[:, qb, :], in0=u_[:, 0:NSEL], scalar=r_[:, 1:2], in1=im_[:, qb, :],
                                                                         op0=ALU.mult, op1=ALU.add),
                                 reads=[uk, rk + 'r', imk + str(qb)], writes=[imk + str(qb)])
                    else:
                        for j, (e_, ek) in enumerate(exs):
                            P.op('pe', lambda e: e.matmul(u_[:, 0:65], lhsT=e_[:, qb * 128:(qb + 1) * 128], rhs=vcaug[:, j, :], start=(j == 0), stop=(j == jmax)),
                                 reads=[ek, 'n_vcaug'], writes=[uk])
                        P.op('act', lambda e: e.copy(out=oc_[:, qb, :], in_=u_[:, 0:65]), reads=[uk], writes=[ock + str(qb)])
            qo_, qok = qt[4]
            for qb in range(4):
                qi = 4 * T + qb
                qsl = slice(qb * 128, (qb + 1) * 128)
                imq = im_[:, qb, :]
                imqk = imk + str(qb)
                for hf in range(2):
                    n0 = 2 * qi + hf
                    ps_ = slice(64 * hf, 64 * hf + 64)
                    P.op('pool', lambda e: e.affine_select(out=im_[ps_, qb, :], in_=im_[ps_, qb, :], pattern=[[-1, NSEL]], compare_op=ALU.is_ge,
                                                           fill=fneg, base=n0, channel_multiplier=0), reads=[imqk], writes=[imqk])
                    P.op('pool', lambda e: e.affine_select(out=im_[ps_, qb, :], in_=im_[ps_, qb, :], pattern=[[1, NSEL]], compare_op=ALU.not_equal,
                                                           fill=fpos, base=-n0, channel_multiplier=0), reads=[imqk], writes=[imqk])
                t_, tk = t8.next()
                sc_, sck = scr.next()
                P.op('dve', lambda e: e.max(out=t_[:, 0:8], in_=imq), reads=[imqk], writes=[tk])
                P.op('dve', lambda e: e.match_replace(out=sc_[:], in_to_replace=t_[:, 0:8], in_values=imq, imm_value=NEG_BIG),
                     reads=[imqk, tk], writes=[sck])
                P.op('dve', lambda e: e.max(out=t_[:, 8:16], in_=sc_[:]), reads=[sck], writes=[tk + 'b'])
                s_, sk = sel.next()
                P.op('dve', lambda e: e.tensor_scalar(out=s_[:], in0=imq, scalar1=t_[:, 15:16], scalar2=1.0, op0=ALU.is_ge, op1=ALU.subtract),
                     reads=[imqk, tk + 'b'], writes=[sk])
                nkt = qi // 4 + 1
                for kt in range(nkt):
                    z_, zk = zps.next()
                    P.op('pe', lambda e: e.matmul(z_[:], lhsT=qo_[:, qsl], rhs=ksaug[:, kt * 512:(kt + 1) * 512], start=True, stop=True),
                         reads=[qok, 'n_ksaug'], writes=[zk])
                    m_, mk = zm.next()
                    P.op('dve', lambda e: e.scalar_tensor_tensor(
                        out=m_[:].rearrange("p (a b) -> p a b", b=64),
                        in0=s_[:, kt * 8:(kt + 1) * 8].unsqueeze(2).to_broadcast([128, 8, 64]), scalar=-NEG_BIG,
                        in1=z_[:].rearrange("p (a b) -> p a b", b=64), op0=ALU.mult, op1=ALU.add),
                        reads=[sk, zk], writes=[mk])
                    if kt == nkt - 1:
                        P.op('pool', lambda e: e.tensor_tensor(out=m_[:], in0=m_[:], in1=smask[:, qi % 4, :], op=ALU.add), reads=[mk, 'n_smask'], writes=[mk])
                    p_, pk = pb.next()
                    P.op('act', lambda e: e.activation(out=p_[:], in_=m_[:], func=AF.Exp, scale=0.125), reads=[mk], writes=[pk])
                    tp_, tpk = tps.next()
                    for j in range(4):
                        P.op('pe', lambda e, j=j: e.transpose(out=tp_[:, j, :], in_=p_[:, j * 128:(j + 1) * 128], identity=identb[:]),
                             reads=[pk, 'identb'], writes=[tpk])
                    pt_, ptk = pT.next()
                    if kt % 2 == 0:
                        P.op('dve', lambda e: e.tensor_copy(out=pt_[:], in_=tp_[:]), reads=[tpk], writes=[ptk])
                    else:
                        P.op('act', lambda e: e.copy(out=pt_[:], in_=tp_[:]), reads=[tpk], writes=[ptk])
                    for j in range(4):
                        P.op('pe', lambda e, j=j: e.matmul(osp[:], lhsT=pt_[:, j, :], rhs=vsaug[:, kt * 4 + j, :],
                                                           start=(kt == 0 and j == 0), stop=(kt == nkt - 1 and j == 3)),
                             reads=[ptk, 'n_vsaug'], writes=['n_osp'])
                kbs = [kb for kb in range(qi - 4, qi + 1) if kb >= 0]
                for idx, kb in enumerate(kbs):
                    z_, zk = zps.next()
                    P.op('pe', lambda e: e.matmul(z_[:, 0:128], lhsT=kwaug[:, kb * 128:(kb + 1) * 128], rhs=qo_[:, qsl], start=True, stop=True),
                         reads=['n_kwaug', qok], writes=[zk])
                    e_, ek = we.next()
                    if kb == qi or kb == qi - 4:
                        w_, wk_ = wz.next()
                        msk = wm0 if kb == qi else wm4
                        P.op('dve', lambda e: e.tensor_tensor(out=w_[:], in0=z_[:, 0:128], in1=msk[:], op=ALU.add), reads=[zk, 'n_wm0', 'n_wm4'], writes=[wk_])
                        P.op('act', lambda e: e.activation(out=e_[:], in_=w_[:], func=AF.Exp, scale=0.125), reads=[wk_], writes=[ek])
                    else:
                        P.op('act', lambda e: e.activation(out=e_[:], in_=z_[:, 0:128], func=AF.Exp, scale=0.125), reads=[zk], writes=[ek])
                    P.op('pe', lambda e: e.matmul(owp[:], lhsT=e_[:], rhs=vwaug[:, kb, :], start=(idx == 0), stop=(idx == len(kbs) - 1)),
                         reads=[ek, 'n_vwaug'], writes=['n_owp'])
                w3, w3k = wq3.next()
                P.op('dve', lambda e: e.tensor_scalar(out=w3[:, 0:1], in0=oc_[:, qb, 64:65], scalar1=1e-30, scalar2=None, op0=ALU.max),
                     reads=[ock + str(qb)], writes=[w3k])
                P.op('dve', lambda e: e.tensor_copy(out=w3[:, 1:2], in_=osp[:, 64:65]), reads=['n_osp'], writes=[w3k])
                P.op('dve', lambda e: e.tensor_copy(out=w3[:, 2:3], in_=owp[:, 64:65]), reads=['n_owp'], writes=[w3k])
                P.op('dve', lambda e: e.reciprocal(out=w3[:, 4:7], in_=w3[:, 0:3]), reads=[w3k], writes=[w3k + 'r'])
                P.op('dve', lambda e: e.tensor_tensor(out=w3[:, 4:7], in0=w3[:, 4:7], in1=gate[:, qi, :], op=ALU.mult), reads=[w3k + 'r', 'n_gate'], writes=[w3k + 'r'])
                o_, obk = ob.next()
                P.op('dve', lambda e: e.tensor_scalar(out=o_[:], in0=oc_[:, qb, 0:64], scalar1=w3[:, 4:5], scalar2=None, op0=ALU.mult),
                     reads=[ock + str(qb), w3k + 'r'], writes=[obk])
                P.op('dve', lambda e: e.scalar_tensor_tensor(out=o_[:], in0=osp[:, 0:64], scalar=w3[:, 5:6], in1=o_[:], op0=ALU.mult, op1=ALU.add),
                     reads=['n_osp', w3k + 'r', obk], writes=[obk])
                P.op('dve', lambda e: e.scalar_tensor_tensor(out=o_[:], in0=owp[:, 0:64], scalar=w3[:, 6:7], in1=o_[:], op0=ALU.mult, op1=ALU.add),
                     reads=['n_owp', w3k + 'r', obk], writes=[obk])
                P.dma('sp', o_b[qi * 128:(qi + 1) * 128, :], o_[:], reads=[obk], group=obk)
        barrier(P)


def build_mix(S, parts=('a', 'b', 'c', 'd')):
    nc = bass.Bass("TRN2", target_bir_lowering=False)
    dr = lambda name, shape, kind="ExternalInput": nc.dram_tensor(name, shape, F32, kind=kind).ap()
    cst = {k: dr('c_' + k, list(v.shape)) for k, v in mix_consts().items()}
    with contextlib.ExitStack() as st0:
        P = Prog(nc, st0)
        if 'a' in parts:
            emit_sb(P, nc, S, dr("a_qT", [64, S]), dr("a_kT", [64, S]), dr("a_v", [S, 64]),
                    dr("o_a", [64, S], kind="ExternalOutput"), cst)
            P.stack = st0
        epsb = P.sb("epsb", [128, 1], F32)
        P.op('pool', lambda e: e.memset(epsb[:], LN_EPS), writes=['epsb'])
        if 'b' in parts:
            identf = P.sb("identf", [128, 128], F32)
            identb = P.sb("identb", [128, 128], BF16)
            P.dma('sp', identf[:], cst['ident'], writes=['identf'], group='const')
            P.op('dve', lambda e: e.tensor_copy(out=identb[:], in_=identf[:]), reads=['identf'], writes=['identb'])
            ncst = {k: dr('c_' + k, list(v.shape)) for k, v in nsa_consts(S, 0).items()}
            dd = dict(b_qT4=dr("b_qT4", [4, 64, S]), b_qTo=dr("b_qTo", [64, S]), b_kcT=dr("b_kcT", [64, S]), b_vcT=dr("b_vcT", [64, S]),
                      b_ksT=dr("b_ksT", [64, S]), b_vs=dr("b_vs", [S, 64]), b_kwT=dr("b_kwT", [64, S]), b_vw=dr("b_vw", [S, 64]),
                      b_g=dr("b_g", [S, 3]), cmp_wk=dr("cmp_wk", [2048, 64]), cmp_wv=dr("cmp_wv", [2048, 64]), cmp_posT=dr("cmp_posT", [64, 32]))
            emit_nsa(P, nc, S, dd, dr("o_b", [S, 64], kind="ExternalOutput"), ncst, identb)
            P.stack = st0
        if 'c' in parts:
            hc = {k: dr('c_' + k, list(v.shape)) for k, v in head_consts(0).items()}
            emit_ret(P, nc, S, dr("c_qT", [64, S]), dr("c_kT", [64, S]), dr("c_k", [S, 64]), dr("c_v", [S, 64]), dr("c_g", [S, 64]),
                     dr("gn_g", [64]), dr("gn_b", [64]), dr("o_c", [S, 64], kind="ExternalOutput"), hc, epsb)
            P.stack = st0
        if 'd' in parts:
            gc = {k: dr('c_' + k, list(v.shape)) for k, v in gla_consts().items()}
            emit_gla(P, nc, S, dr("d_qT", [32, S]), dr("d_kT", [32, S]), dr("d_k", [S, 32]), dr("d_v", [S, 64]), dr("d_r", [S, 64]),
                     dr("d_lrT", [16, S]), dr("w_al", [16, 32]), dr("b_al", [32]), dr("norm_g", [64]),
                     dr("o_d", [S, 64], kind="ExternalOutput"), gc, epsb)
        P.finish('sp')
    return nc


_W = np.cumsum((0,) + (256, 256, 256, 256, 64, 64, 64, 64, 64, 64, 12, 256, 256, 256, 256, 128, 128, 256, 256, 16, 4096))


def mix_inputs(Pm, inp, l, h, parts):
    def col(i, w=64, hh=h):
        c0 = _W[i] + (0 if hh is None else hh * w)
        return Pm[:, c0:c0 + w]
    T_ = lambda a: np.ascontiguousarray(a.T)
    C_ = np.ascontiguousarray
    m = {('c_' + k): v for k, v in mix_consts().items()}
    if 'a' in parts:
        m.update(a_qT=T_(col(0)), a_kT=T_(col(1)), a_v=C_(col(2)))
    if 'b' in parts:
        S_ = Pm.shape[0]
        m.update({('c_' + k): v for k, v in nsa_consts(S_, h).items()})
        m.update(b_qT4=C_(col(3, 256, 0).reshape(S_, 4, 64).transpose(1, 2, 0)), b_qTo=T_(col(3)),
                 b_kcT=T_(col(4, 64, 0)), b_vcT=T_(col(5, 64, 0)), b_ksT=T_(col(6, 64, 0)), b_vs=C_(col(7, 64, 0)),
                 b_kwT=T_(col(8, 64, 0)), b_vw=C_(col(9, 64, 0)), b_g=C_(Pm[:, _W[10] + 3 * h:_W[10] + 3 * h + 3]),
                 cmp_wk=C_(inp['nsa_cmp_wk'][l]), cmp_wv=C_(inp['nsa_cmp_wv'][l]), cmp_posT=T_(inp['nsa_cmp_pos'][l]))
    if 'c' in parts:
        m.update({('c_' + k): v for k, v in head_consts(h).items()})
        m.update(c_qT=T_(col(11)), c_kT=T_(col(12)), c_k=C_(col(12)), c_v=C_(col(13)), c_g=C_(col(14)),
                 gn_g=C_(inp['ret_gn_g'][l][h * 64:(h + 1) * 64]), gn_b=C_(inp['ret_gn_b'][l][h * 64:(h + 1) * 64]))
    if 'd' in parts:
        m.update({('c_' + k): v for k, v in gla_consts().items()})
        m.update(d_qT=T_(col(15, 32)), d_kT=T_(col(16, 32)), d_k=C_(col(16, 32)), d_v=C_(col(17)), d_r=C_(col(18)),
                 d_lrT=T_(col(19, 16, None)), w_al=C_(inp['gla_w_alpha'][l][:, h * 32:(h + 1) * 32]),
                 b_al=C_(inp['gla_b_alpha'][l][h * 32:(h + 1) * 32]), norm_g=C_(inp['gla_norm_g'][l][h * 64:(h + 1) * 64]))
    return m


NMIX = 3228


def kernel(**inp):
    f = lambda k: np.ascontiguousarray(np.asarray(inp[k], dtype=np.float32))
    x = f('x').reshape(BATCH * SEQ, D)
    h = x
    nc_mix = build_mix(SEQ)
    nc_tail = build_tail(BATCH * SEQ // NCORES)
    TT = BATCH * SEQ // NCORES
    for l in range(2):
        w_in = f('w_in')[l]
        if l == 0:
            h, Pm = run_proj(h, np.ascontiguousarray(w_in[:, :NMIX]), f('ln_in_g'), f('ln_in_b'))
        else:
            h, Pm = run_proj(h, np.ascontiguousarray(w_in[:, :NMIX]))
        Pm = Pm.reshape(BATCH, SEQ, NMIX)
        in_maps = [mix_inputs(Pm[c // 4], inp, l, c % 4, 'abcd') for c in range(NCORES)]
        res = run_bass_kernel_spmd(nc_mix, in_maps, core_ids=list(range(NCORES)))
        brT = np.empty((4, 4, 64, BATCH, SEQ), np.float32)
        for c in range(NCORES):
            r = res.results[c]
            b_, h_ = c // 4, c % 4
            brT[0, h_, :, b_, :] = r['o_a']
            brT[1, h_, :, b_, :] = r['o_b'].T
            brT[2, h_, :, b_, :] = r['o_c'].T
            brT[3, h_, :, b_, :] = r['o_d'].T
        brT = brT.reshape(D, BATCH * SEQ)
        consts = dict(
            wg=np.ascontiguousarray(w_in[:, NMIX:]), wbr=np.ascontiguousarray(f('w_branch')[l].reshape(D, D)),
            wout=f('w_out')[l], ln1g=f('ln1_g')[l], ln1b=f('ln1_b')[l], ln2g=f('ln2_g')[l], ln2b=f('ln2_b')[l],
            wq=f('peer_wq')[l],
            keysT=np.ascontiguousarray(f('peer_keys')[l].reshape(16, 128, 128).transpose(2, 0, 1).reshape(128, 2048)),
            UT=np.ascontiguousarray(f('peer_u')[l].T), V=f('peer_v')[l], ident=_IDENT)
        in_maps = []
        for c in range(NCORES):
            sl = slice(c * TT, (c + 1) * TT)
            m = dict(consts)
            m['x'] = np.ascontiguousarray(h[sl])
            m['brT'] = np.ascontiguousarray(brT[:, sl])
            in_maps.append(m)
        res = run_bass_kernel_spmd(nc_tail, in_maps, core_ids=list(range(NCORES)))
        h = np.concatenate([r['out'] for r in res.results], axis=0)
    return h.reshape(BATCH, SEQ, D)
```
